# Optimizing a Trainium2 kernel written in Bass

```python
import math
import jax, jax.numpy as jnp
from jax import lax
import numpy as np

D_MODEL = 2048
BATCH = 4
SEQ = 8192
DEPTH = 2

N_EVEN = (DEPTH + 1) // 2
N_ODD = DEPTH // 2

CONV_DIM = D_MODEL // 2
CONV_GROUPS = 16
CONV_WIDTH = 3
DIFF_HEADS = 8
DIFF_D = (D_MODEL // 2) // (2 * DIFF_HEADS)
DIFF_V = 2 * DIFF_D
DIFF_QK = DIFF_HEADS * 2 * DIFF_D
HYB_IN = 3 * CONV_DIM + 2 * DIFF_QK + DIFF_HEADS * DIFF_V
HYB_OUT = CONV_DIM + DIFF_HEADS * DIFF_V
FOX_HEADS = 16
FOX_D = D_MODEL // FOX_HEADS
FOX_IN = 3 * D_MODEL + FOX_HEADS
D_FF = 4 * D_MODEL

Q_BLOCK = 128
EPS = 1e-6

kernel_name = "hybrid_shortconv_diffattn_fox_sqrelu"


def _rmsnorm(x, g):
    xf = x.astype(jnp.float32)
    y = xf * lax.rsqrt(jnp.mean(xf * xf, axis=-1, keepdims=True) + EPS)
    return (y * g.astype(jnp.float32)).astype(x.dtype)


def _split(a, sizes):
    idx = [int(s) for s in np.cumsum(sizes)[:-1]]
    return jnp.split(a, idx, axis=-1)


def _causal_conv(u, w):
    c = u.shape[-1]
    return lax.conv_general_dilated(
        u, w[:, None, :].astype(u.dtype), window_strides=(1,),
        padding=[(CONV_WIDTH - 1, 0)], dimension_numbers=("NWC", "WIO", "NWC"),
        feature_group_count=c)


def _causal_mask(i, seq):
    q_pos = i * Q_BLOCK + jnp.arange(Q_BLOCK)
    k_pos = jnp.arange(seq)
    return k_pos[None, :] <= q_pos[:, None]


def _block_sweep(fn, seq):
    out = lax.map(fn, jnp.arange(seq // Q_BLOCK))
    nb, b, qb, h, e = out.shape
    return jnp.moveaxis(out, 0, 1).reshape(b, nb * qb, h, e)


def _diff_attention(q, k, v, lam, seq):
    scale = DIFF_D ** -0.5

    def block(i):
        qb = lax.dynamic_slice_in_dim(q, i * Q_BLOCK, Q_BLOCK, axis=1)
        s = jnp.einsum("bqhmd,bkhmd->bhmqk", qb, k,
                       preferred_element_type=jnp.float32) * scale
        s = jnp.where(_causal_mask(i, seq), s, -jnp.inf)
        p = jax.nn.softmax(s, axis=-1)
        pd = p[:, :, 0] - lam * p[:, :, 1]
        return jnp.einsum("bhqk,bkhe->bqhe", pd.astype(v.dtype), v)

    return _block_sweep(block, seq)


def _forgetting_attention(q, k, v, logf, seq):
    c = jnp.transpose(jnp.cumsum(logf, axis=1), (0, 2, 1))
    scale = FOX_D ** -0.5

    def block(i):
        qb = lax.dynamic_slice_in_dim(q, i * Q_BLOCK, Q_BLOCK, axis=1)
        c_q = lax.dynamic_slice_in_dim(c, i * Q_BLOCK, Q_BLOCK, axis=2)
        s = jnp.einsum("bqhd,bkhd->bhqk", qb, k,
                       preferred_element_type=jnp.float32) * scale
        s = s + (c_q[..., :, None] - c[..., None, :])
        s = jnp.where(_causal_mask(i, seq), s, -jnp.inf)
        p = jax.nn.softmax(s, axis=-1)
        return jnp.einsum("bhqk,bkhd->bqhd", p.astype(v.dtype), v)

    return _block_sweep(block, seq)


def _conv_diff_mixer(h, w_in, conv_w, dq_g, dk_g, lq1, lk1, lq2, lk2, subln_g, w_out, layer_idx):
    b, t, _ = h.shape
    proj = h @ w_in
    gate_b, gate_c, u, q, k, v = _split(
        proj, [CONV_DIM, CONV_DIM, CONV_DIM, DIFF_QK, DIFF_QK, DIFF_HEADS * DIFF_V])
    y_conv = gate_b * _causal_conv(gate_c * u, conv_w)
    q = _rmsnorm(q.reshape(b, t, DIFF_HEADS, 2, DIFF_D), dq_g)
    k = _rmsnorm(k.reshape(b, t, DIFF_HEADS, 2, DIFF_D), dk_g)
    v = v.reshape(b, t, DIFF_HEADS, DIFF_V)
    lam_init = 0.8 - 0.6 * math.exp(-0.3 * layer_idx)
    f32 = jnp.float32
    lam = (jnp.exp(jnp.sum(lq1.astype(f32) * lk1.astype(f32)))
           - jnp.exp(jnp.sum(lq2.astype(f32) * lk2.astype(f32))) + lam_init)
    o = _diff_attention(q, k, v, lam, t)
    o = _rmsnorm(o, subln_g) * (1.0 - lam_init)
    o = o.astype(h.dtype).reshape(b, t, DIFF_HEADS * DIFF_V)
    return jnp.concatenate([y_conv, o], axis=-1) @ w_out


def _fox_mixer(h, w_in, b_f, q_g, k_g, w_out):
    b, t, _ = h.shape
    proj = h @ w_in
    q, k, v, f = _split(proj, [D_MODEL, D_MODEL, D_MODEL, FOX_HEADS])
    q = _rmsnorm(q.reshape(b, t, FOX_HEADS, FOX_D), q_g)
    k = _rmsnorm(k.reshape(b, t, FOX_HEADS, FOX_D), k_g)
    v = v.reshape(b, t, FOX_HEADS, FOX_D)
    logf = jax.nn.log_sigmoid(f.astype(jnp.float32) + b_f.astype(jnp.float32))
    o = _forgetting_attention(q, k, v, logf, t)
    return o.reshape(b, t, D_MODEL) @ w_out


def _sqrelu_mlp(h, w1, w2):
    return jnp.square(jax.nn.relu(h @ w1)) @ w2


def setup_inputs(seed: int = 0) -> dict:
    key = jax.random.key(seed)
    ks = jax.random.split(key, 20)
    nrm = jax.random.normal
    f32 = jnp.float32
    D = D_MODEL
    return {
        "x": nrm(ks[0], (BATCH, SEQ, D), f32),
        "norm1_g": 1.0 + 0.02 * nrm(ks[1], (DEPTH, D), f32),
        "norm2_g": 1.0 + 0.02 * nrm(ks[2], (DEPTH, D), f32),
        "hyb_w_in": nrm(ks[3], (N_EVEN, D, HYB_IN), f32) * D ** -0.5,
        "hyb_conv_w": nrm(ks[4], (N_EVEN, CONV_WIDTH, CONV_DIM), f32) * CONV_WIDTH ** -0.5,
        "hyb_dq_g": 1.0 + 0.02 * nrm(ks[5], (N_EVEN, DIFF_D), f32),
        "hyb_dk_g": 1.0 + 0.02 * nrm(ks[6], (N_EVEN, DIFF_D), f32),
        "hyb_lq1": 0.1 * nrm(ks[7], (N_EVEN, DIFF_D), f32),
        "hyb_lk1": 0.1 * nrm(ks[8], (N_EVEN, DIFF_D), f32),
        "hyb_lq2": 0.1 * nrm(ks[9], (N_EVEN, DIFF_D), f32),
        "hyb_lk2": 0.1 * nrm(ks[10], (N_EVEN, DIFF_D), f32),
        "hyb_subln_g": 1.0 + 0.02 * nrm(ks[11], (N_EVEN, DIFF_V), f32),
        "hyb_w_out": nrm(ks[12], (N_EVEN, HYB_OUT, D), f32) * HYB_OUT ** -0.5,
        "fox_w_in": nrm(ks[13], (N_ODD, D, FOX_IN), f32) * D ** -0.5,
        "fox_b_f": 3.0 + 0.5 * nrm(ks[14], (N_ODD, FOX_HEADS), f32),
        "fox_q_g": 1.0 + 0.02 * nrm(ks[15], (N_ODD, FOX_D), f32),
        "fox_k_g": 1.0 + 0.02 * nrm(ks[16], (N_ODD, FOX_D), f32),
        "fox_w_out": nrm(ks[17], (N_ODD, D, D), f32) * D ** -0.5,
        "mlp_w1": nrm(ks[18], (DEPTH, D, D_FF), f32) * D ** -0.5,
        "mlp_w2": nrm(ks[19], (DEPTH, D_FF, D), f32) * D_FF ** -0.5,
    }


def reference(x, norm1_g, norm2_g, hyb_w_in, hyb_conv_w, hyb_dq_g, hyb_dk_g,
              hyb_lq1, hyb_lk1, hyb_lq2, hyb_lk2, hyb_subln_g, hyb_w_out,
              fox_w_in, fox_b_f, fox_q_g, fox_k_g, fox_w_out, mlp_w1, mlp_w2):
    for l in range(DEPTH):
        h = _rmsnorm(x, norm1_g[l])
        j = l // 2
        if l % 2 == 0:
            x = x + _conv_diff_mixer(h, hyb_w_in[j], hyb_conv_w[j], hyb_dq_g[j], hyb_dk_g[j],
                                     hyb_lq1[j], hyb_lk1[j], hyb_lq2[j], hyb_lk2[j],
                                     hyb_subln_g[j], hyb_w_out[j], l)
        else:
            x = x + _fox_mixer(h, fox_w_in[j], fox_b_f[j], fox_q_g[j], fox_k_g[j], fox_w_out[j])
        h = _rmsnorm(x, norm2_g[l])
        x = x + _sqrelu_mlp(h, mlp_w1[l], mlp_w2[l])
    return x
```

```python
import math
from contextlib import ExitStack

import numpy as np
import concourse.bass as bass
import concourse.mybir as mybir
from concourse.bass_utils import run_bass_kernel_spmd

F32 = mybir.dt.float32
BF16 = mybir.dt.bfloat16
AF = mybir.ActivationFunctionType
ALU = mybir.AluOpType
AX = mybir.AxisListType

D = 2048
DFF = 8192
EPS = 1e-6
SEM_CAP = 30000
DMA_RING = 6


class Buf:
    __slots__ = ("name", "last_w", "readers")

    def __init__(self, name=""):
        self.name = name
        self.last_w = None
        self.readers = []


class Op:
    __slots__ = ("eng", "method", "args", "kw", "reads", "writes", "is_dma", "deps",
                 "event", "signal", "bar_last")

    def __init__(self, eng, method, args, kw, reads, writes, is_dma):
        self.eng = eng
        self.method = method
        self.args = args
        self.kw = kw
        self.reads = reads
        self.writes = writes
        self.is_dma = is_dma
        self.deps = None
        self.event = None
        self.signal = False
        self.bar_last = None


class Sched:
    ENGS = ("pe", "act", "dve", "pool", "sp")
    COMPUTE = ("pe", "act", "dve", "pool")

    def __init__(self, nc, same_engine_sync=True):
        self.nc = nc
        self.eng = {"pe": nc.tensor, "act": nc.scalar, "dve": nc.vector,
                    "pool": nc.gpsimd, "sp": nc.sync}
        self.ops = []
        self.same_engine_sync = same_engine_sync
        self.cur_sem = {}
        self.cur_cnt = {}
        self.waited = {e: {} for e in self.ENGS}
        self.dma_ring = {}
        self.dma_next = {}
        self.nsem = 0
        self.n_emitted = {e: 0 for e in self.ENGS}
        self.n_waits = {e: 0 for e in self.ENGS}

    def op(self, eng, method, args=(), kw=None, reads=(), writes=()):
        o = Op(eng, method, args, kw or {}, list(reads), list(writes), False)
        self.ops.append(o)
        return o

    def dma(self, eng, out_ap, in_ap, reads=(), writes=(), **kw):
        o = Op(eng, "dma_start", (), dict(out=out_ap, in_=in_ap, **kw), list(reads), list(writes), True)
        self.ops.append(o)
        return o

    def barrier(self):
        o = Op(None, "barrier", (), {}, [], [], False)
        self.ops.append(o)

    def _new_sem(self, name):
        self.nsem += 1
        return self.nc.alloc_semaphore(f"{name}_{self.nsem}")

    def _eng_event(self, e):
        if e not in self.cur_sem or self.cur_cnt[e] >= SEM_CAP:
            self.cur_sem[e] = self._new_sem(f"s_{e}")
            self.cur_cnt[e] = 0
        self.cur_cnt[e] += 1
        return (self.cur_sem[e], self.cur_cnt[e])

    def _wait(self, e, sem, val):
        w = self.waited[e]
        k = id(sem)
        if w.get(k, (None, 0))[1] >= val:
            return
        self.eng[e].wait_ge(sem, val)
        self.n_waits[e] += 1
        w[k] = (sem, val)

    def flush(self):
        ops = self.ops
        self.ops = []
        last = {}
        for o in ops:
            if o.method == "barrier":
                o.bar_last = dict(last)
                for d in last.values():
                    d.signal = True
                continue
            deps = []
            for b in o.reads:
                if b.last_w is not None:
                    deps.append(b.last_w)
            for b in o.writes:
                if b.last_w is not None:
                    deps.append(b.last_w)
                deps.extend(b.readers)
            o.deps = deps
            for b in o.reads:
                b.readers.append(o)
            for b in o.writes:
                b.last_w = o
                b.readers = []
            for d in deps:
                if d.is_dma:
                    continue
                if d.eng != o.eng or (self.same_engine_sync and d.eng != "pe"):
                    d.signal = True
            if not o.is_dma:
                last[o.eng] = o
        for o in ops:
            if o.method == "barrier":
                evs = [d.event for d in o.bar_last.values()]
                for ring in self.dma_ring.values():
                    for slot in ring:
                        if slot[1] > 0:
                            evs.append((slot[0], slot[1]))
                for e in self.ENGS:
                    for sem, val in evs:
                        self._wait(e, sem, val)
                continue
            e = o.eng
            engine = self.eng[e]
            waits = {}
            for d in o.deps:
                if d.event is None:
                    continue
                if (not d.is_dma) and d.eng == e and (e == "pe" or not self.same_engine_sync):
                    continue
                sem, val = d.event
                k = id(sem)
                if waits.get(k, (None, 0))[1] < val:
                    waits[k] = (sem, val)
            if o.is_dma:
                if e not in self.dma_ring:
                    self.dma_ring[e] = [[self._new_sem(f"d_{e}"), 0] for _ in range(DMA_RING)]
                ring = self.dma_ring[e]
                j = self.dma_next.get(e, 0)
                self.dma_next[e] = j + 1
                slot = ring[j % DMA_RING]
                if slot[1] > 0:
                    k = id(slot[0])
                    if waits.get(k, (None, 0))[1] < slot[1]:
                        waits[k] = (slot[0], slot[1])
            for sem, val in waits.values():
                self._wait(e, sem, val)
            ins = getattr(engine, o.method)(*o.args, **o.kw)
            self.n_emitted[e] += 1
            if o.is_dma:
                slot[1] += 16
                ins.then_inc(slot[0], 16)
                o.event = (slot[0], slot[1])
            elif o.signal:
                ev = self._eng_event(e)
                ins.then_inc(ev[0], 1)
                o.event = ev
            o.deps = None
            o.args = None
            o.kw = None

    def finish(self, final_bufs):
        self.flush()
        sp = self.eng["sp"]
        for b in final_bufs:
            o = b.last_w
            if o is None:
                continue
            sem, val = o.event
            sp.wait_ge(sem, val)


class Ring:
    def __init__(self, es, nc, name, shape, dt, n, psum=False):
        self.items = []
        for i in range(n):
            mk = nc.psum_tensor if psum else nc.sbuf_tensor
            t = es.enter_context(mk(f"{name}{i}", shape, dt))
            self.items.append((t, Buf(f"{name}{i}")))
        self.k = 0

    def get(self):
        it = self.items[self.k % len(self.items)]
        self.k += 1
        return it


class Stream:
    def __init__(self, S, ring, srcs, depth, bufs_fn=None):
        self.S, self.ring, self.srcs, self.depth = S, ring, srcs, depth
        self.issued = []
        self.k = 0

    def _issue(self):
        i = len(self.issued)
        t, b = self.ring.get()
        self.S.dma("sp", t[:], self.srcs[i], writes=[b])
        self.issued.append((t, b))

    def next(self):
        while len(self.issued) < min(len(self.srcs), self.k + 1 + self.depth):
            self._issue()
        it = self.issued[self.k]
        self.k += 1
        return it


WSPECS = [
    ("hyb_w_in", D, 6144), ("hyb_w_out", D, D), ("fox_w_in", D, 6160), ("fox_w_out", D, D),
]


def build(T, dbg=False, phases=None):
    nc = bass.Bass("TRN2", target_bir_lowering=False)
    NT = T // 512
    NB = T // 128

    def din(name, shape):
        return nc.dram_tensor(name, shape, F32, kind="ExternalInput").ap()

    x_in = din("x", [T, D])
    norm1_g = din("norm1_g", [2, D])
    norm2_g = din("norm2_g", [2, D])
    hyb_w_in = din("hyb_w_in", [1, D, 6144])
    hyb_conv_w = din("hyb_conv_w", [1, 3, 1024])
    hyb_dq_g = din("hyb_dq_g", [1, 64])
    hyb_dk_g = din("hyb_dk_g", [1, 64])
    hyb_lq1 = din("hyb_lq1", [1, 64])
    hyb_lk1 = din("hyb_lk1", [1, 64])
    hyb_lq2 = din("hyb_lq2", [1, 64])
    hyb_lk2 = din("hyb_lk2", [1, 64])
    hyb_subln_g = din("hyb_subln_g", [1, 128])
    hyb_w_out = din("hyb_w_out", [1, D, D])
    fox_w_in = din("fox_w_in", [1, D, 6160])
    fox_b_f = din("fox_b_f", [1, 16])
    fox_q_g = din("fox_q_g", [1, 128])
    fox_k_g = din("fox_k_g", [1, 128])
    fox_w_out = din("fox_w_out", [1, D, D])
    mlp_w1 = din("mlp_w1", [2, D, DFF])
    mlp_w2 = din("mlp_w2", [2, DFF, D])
    cst = din("cst", [128, 4, 128])
    out = nc.dram_tensor("out", [T, D], F32, kind="ExternalOutput").ap()

    def scr(name, shape, dt=BF16):
        return nc.dram_tensor(name, shape, dt).ap()

    wc_hin = scr("wc_hin", [1, 12, 128, 16, 512])
    wc_hout = scr("wc_hout", [1, 4, 128, 16, 512])
    wc_fin = scr("wc_fin", [1, 12, 128, 16, 512])
    wc_ff = scr("wc_ff", [128, 16, 16])
    wc_fout = scr("wc_fout", [1, 4, 128, 16, 512])
    wc_w1 = [scr(f"wc_w1_{l}", [1, 16, 128, 16, 512]) for l in range(2)]
    wc_w2 = [scr(f"wc_w2_{l}", [4, 4, 128, 16, 512]) for l in range(2)]
    qT_s = scr("qT_s", [2048, T])
    kT_s = scr("kT_s", [2048, T])
    v_s = scr("v_s", [16, 128, NB, 128])
    aT_s = scr("aT_s", [NT, 128, 16, 512])
    x1_s = scr("x1_s", [T, D], F32)
    dbg_out = {}

    S = Sched(nc)
    ges = ExitStack()

    def gsb(name, shape, dt):
        return ges.enter_context(nc.sbuf_tensor(name, shape, dt))

    c_bf = gsb("c_bf", [128, 4, 128], BF16)
    c_f32 = gsb("c_f32", [128, 4, 128], F32)
    B_c = Buf("consts")
    S.dma("sp", c_f32[:], cst[:, :, :], writes=[B_c])
    S.op("dve", "tensor_copy", (c_bf[:], c_f32[:]), reads=[B_c], writes=[B_c])
    trineg = gsb("trineg", [128, 128], BF16)
    S.op("dve", "tensor_scalar", (trineg[:], c_f32[:, 1, :], -1.0, 30000.0, ALU.add, ALU.mult),
         reads=[B_c], writes=[B_c])
    ident = c_bf[:, 0, :]
    tri_bf = c_bf[:, 1, :]
    ones_bf = c_bf[:, 2, :]
    blk64_bf = c_bf[:, 3, :]
    ident32 = c_f32[:, 0, :]
    tri32 = c_f32[:, 1, :]
    ones32 = c_f32[:, 2, :]
    LT = gsb("LT", [128, NB, 16], F32)
    Lref = gsb("Lref", [128, NT, 16], F32)
    B_LT = Buf("LT")
    B_Lref = Buf("Lref")
    prm = gsb("prm", [128, 16], F32)
    B_prm = Buf("prm")
    PQ0, PK0, PSUB, PQ1, PK1, PNLAM = range(6)

    def col_load(dst_ap, src_row_ap, n):
        S.dma("sp", dst_ap, src_row_ap.rearrange("o d -> d o"), writes=[B_prm])

    col_load(prm[0:64, PQ0:PQ0 + 1], hyb_dq_g[0:1, :], 64)
    col_load(prm[64:128, PQ0:PQ0 + 1], hyb_dq_g[0:1, :], 64)
    col_load(prm[0:64, PK0:PK0 + 1], hyb_dk_g[0:1, :], 64)
    col_load(prm[64:128, PK0:PK0 + 1], hyb_dk_g[0:1, :], 64)
    col_load(prm[:, PSUB:PSUB + 1], hyb_subln_g[0:1, :], 128)
    col_load(prm[:, PQ1:PQ1 + 1], fox_q_g[0:1, :], 128)
    col_load(prm[:, PK1:PK1 + 1], fox_k_g[0:1, :], 128)
    lam_init = 0.8 - 0.6 * math.exp(-0.3 * 0)
    S.op("dve", "tensor_scalar", (prm[:, PQ0:PQ0 + 1], prm[:, PQ0:PQ0 + 1], 64 ** -0.5, None, ALU.mult),
         reads=[B_prm], writes=[B_prm])
    S.op("dve", "tensor_scalar", (prm[:, PSUB:PSUB + 1], prm[:, PSUB:PSUB + 1], 1.0 - lam_init, None, ALU.mult),
         reads=[B_prm], writes=[B_prm])
    S.op("dve", "tensor_scalar", (prm[:, PQ1:PQ1 + 1], prm[:, PQ1:PQ1 + 1], 128 ** -0.5, None, ALU.mult),
         reads=[B_prm], writes=[B_prm])
    lv = gsb("lv", [128, 4, 64], F32)
    lw = gsb("lw", [128, 2, 64], F32)
    le = gsb("le", [128, 4], F32)
    B_l = Buf("lam")
    for i, a in enumerate((hyb_lq1, hyb_lk1, hyb_lq2, hyb_lk2)):
        S.dma("sp", lv[:, i, :], a[0:1, :].partition_broadcast(128), writes=[B_l])
    S.op("dve", "tensor_tensor", (lw[:, 0, :], lv[:, 0, :], lv[:, 1, :], ALU.mult), reads=[B_l], writes=[B_l])
    S.op("dve", "tensor_tensor", (lw[:, 1, :], lv[:, 2, :], lv[:, 3, :], ALU.mult), reads=[B_l], writes=[B_l])
    S.op("dve", "tensor_reduce", (le[:, 0:2], lw[:, :, :], AX.X, ALU.add), reads=[B_l], writes=[B_l])
    S.op("act", "activation", (le[:, 2:4], le[:, 0:2], AF.Exp), reads=[B_l], writes=[B_l])
    S.op("dve", "tensor_tensor", (le[:, 0:1], le[:, 3:4], le[:, 2:3], ALU.subtract), reads=[B_l], writes=[B_l])
    S.op("dve", "tensor_scalar", (prm[:, PNLAM:PNLAM + 1], le[:, 0:1], -lam_init, None, ALU.add),
         reads=[B_l, B_prm], writes=[B_prm])
    bfB = gsb("bfB", [128, 16], F32)
    S.dma("sp", bfB[:], fox_b_f[0:1, :].partition_broadcast(128), writes=[B_prm])
    cw = gsb("cw", [128, 3, 8], F32)
    cwr = gsb("cwr", [24, 128], F32)
    B_cwr = Buf()
    S.dma("sp", cwr[:], hyb_conv_w[0].rearrange("k (c p) -> (k c) p", p=128), writes=[B_cwr])
    with nc.psum_tensor("cw_ps", [128, 24], F32) as cw_ps:
        B_cwps = Buf()
        S.op("pe", "transpose", (cw_ps[:], cwr[:], c_f32[0:24, 0, 0:24]), reads=[B_cwr, B_c], writes=[B_cwps])
        S.op("dve", "tensor_copy", (cw[:].rearrange("p k c -> p (k c)"), cw_ps[:]), reads=[B_cwps], writes=[B_prm])
        S.flush()

    bg_pieces = []
    with ExitStack() as es:
        pieces = []

        def cast_w(dst, src, K, N):
            for kg in range(K // 2048):
                for cb in range(N // 512):
                    for q in range(4):
                        r0 = kg * 2048 + q * 512
                        pieces.append((dst[kg, cb][:, q * 4:(q + 1) * 4, :],
                                       src[r0:r0 + 512, cb * 512:(cb + 1) * 512].rearrange(
                                           "(kc p) c -> p kc c", p=128)))

        cast_w(wc_hin, hyb_w_in[0], D, 6144)
        n_now = len(pieces)
        cast_w(wc_hout, hyb_w_out[0], D, D)
        cast_w(wc_w1[0], mlp_w1[0], D, DFF)
        cast_w(wc_w2[0], mlp_w2[0], DFF, D)
        cast_w(wc_fin, fox_w_in[0], D, 6144)
        cast_w(wc_fout, fox_w_out[0], D, D)
        cast_w(wc_w1[1], mlp_w1[1], D, DFF)
        cast_w(wc_w2[1], mlp_w2[1], DFF, D)
        bg_pieces.extend(pieces[n_now:])
        pieces = pieces[:n_now]
        str_ = Ring(es, nc, "pp_s", [128, 4, 512], F32, 4)
        btr = Ring(es, nc, "pp_b", [128, 4, 512], BF16, 4)
        st = Stream(S, str_, [p_[1] for p_ in pieces], 3)
        engs = [("dve", "tensor_copy"), ("act", "copy"), ("pool", "tensor_copy")]
        for i, (dst, _) in enumerate(pieces):
            s_t, s_b = st.next()
            b_t, b_b = btr.get()
            e, m = engs[i % 3]
            S.op(e, m, (b_t[:], s_t[:]), reads=[s_b], writes=[b_b])
            S.dma("sp", dst, b_t[:], reads=[b_b])
        ffs = es.enter_context(nc.sbuf_tensor("pp_ffs", [128, 16, 16], F32))
        ffb = es.enter_context(nc.sbuf_tensor("pp_ffb", [128, 16, 16], BF16))
        B_ff = Buf()
        for kc in range(16):
            S.dma("sp", ffs[:, kc, :], fox_w_in[0][kc * 128:(kc + 1) * 128, 6144:6160], writes=[B_ff])
        S.op("dve", "tensor_copy", (ffb[:], ffs[:]), reads=[B_ff], writes=[B_ff])
        S.dma("sp", wc_ff[:, :, :], ffb[:], reads=[B_ff])
        S.barrier()
        S.flush()

    def norm_block(es_tag, xt_ap, xt_bufs, gB, hb_ring, junk, Bjunk, sm_ring, tp_ring, hT, hTB, tb):
        sm, Bsm = sm_ring.get()
        S.op("act", "activation", (junk[:], xt_ap, AF.Square), dict(accum_out=sm[:, 0:1]),
             reads=xt_bufs, writes=[Bjunk, Bsm])
        S.op("act", "activation", (sm[:, 1:2], sm[:, 0:1], AF.Sqrt), dict(scale=1.0 / D, bias=EPS),
             reads=[Bsm], writes=[Bsm])
        S.op("dve", "reciprocal", (sm[:, 2:3], sm[:, 1:2]), reads=[Bsm], writes=[Bsm])
        hb, Bhb = hb_ring.get()
        S.op("dve", "scalar_tensor_tensor", (hb[:], xt_ap, sm[:, 2:3], gB[:], ALU.mult, ALU.mult),
             reads=xt_bufs + [Bsm], writes=[Bhb])
        for half in range(2):
            pt, Bpt = tp_ring.get()
            for k in range(8):
                kc = half * 8 + k
                S.op("pe", "transpose", (pt[:, k, :], hb[:, kc * 128:(kc + 1) * 128], ident),
                     reads=[Bhb], writes=[Bpt])
            dst = hT[:, half * 8:(half + 1) * 8, tb * 128:(tb + 1) * 128]
            if half == 0:
                S.op("act", "copy", (dst, pt[:, :, :]), reads=[Bpt], writes=[hTB[tb][half]])
            else:
                S.op("dve", "tensor_copy", (dst, pt[:, :, :]), reads=[Bpt], writes=[hTB[tb][half]])

    def hT_reads(hTB, kc):
        return [hTB[tb][kc // 8] for tb in range(4)]

    def phase_A(l, x_src):
        with ExitStack() as es:
            p = f"A{l}"
            xr = Ring(es, nc, p + "x", [128, D], F32, 3)
            hbr = Ring(es, nc, p + "hb", [128, D], BF16, 2)
            junk = es.enter_context(nc.sbuf_tensor(p + "junk", [128, D], BF16))
            Bjunk = Buf()
            smr = Ring(es, nc, p + "sm", [128, 4], F32, 4)
            hTr = [(es.enter_context(nc.sbuf_tensor(f"{p}hT{i}", [128, 16, 512], BF16)),
                    [[Buf(), Buf()] for _ in range(4)]) for i in range(2)]
            wr = Ring(es, nc, p + "w", [128, 16, 512], BF16, 3)
            gB = es.enter_context(nc.sbuf_tensor(p + "g", [128, D], F32))
            BgB = Buf()
            S.dma("sp", gB[:], norm1_g[l:l + 1, :].partition_broadcast(128), writes=[BgB])
            acc = Ring(es, nc, p + "acc", [128, 512], F32, 5, psum=True)
            tpr = Ring(es, nc, p + "tp", [128, 8, 128], BF16, 2, psum=True)
            msr = Ring(es, nc, p + "ms", [128, 512], F32, 1, psum=True)
            sqr = Ring(es, nc, p + "sq", [128, 512], BF16, 2)
            rtr = Ring(es, nc, p + "rt", [128, 512], F32, 2)
            qnr = Ring(es, nc, p + "qn", [128, 512], BF16, 3)
            vsr = Ring(es, nc, p + "vs", [128, 4, 4, 128], BF16, 2)
            if l == 0:
                gc = es.enter_context(nc.sbuf_tensor(p + "gc", [128, 8, 512], BF16))
                Bgc = [Buf() for _ in range(8)]
                z = es.enter_context(nc.sbuf_tensor(p + "z", [128, 8, 516], F32))
                Bz = [Buf() for _ in range(8)]
                S.op("pool", "memset", (z[:, :, 0:2], 0.0), writes=Bz)
                ctr = Ring(es, nc, p + "ct", [128, 512], F32, 2)
                yr = Ring(es, nc, p + "y", [128, 512], BF16, 3)
                w_tiles = wc_hin
                gq, gk, onesm, inv_d = PQ0, PK0, blk64_bf, 1.0 / 64
            else:
                w_tiles = wc_fin
                gq, gk, onesm, inv_d = PQ1, PK1, ones_bf, 1.0 / 128
                wf = es.enter_context(nc.sbuf_tensor(p + "wf", [128, 16, 16], BF16))
                Bwf = Buf()
                S.dma("sp", wf[:], wc_ff[:, :, :], writes=[Bwf])
                R = es.enter_context(nc.sbuf_tensor(p + "R", [128, 16], F32))
                BR = Buf()
                S.op("pool", "memset", (R[:], 0.0), writes=[BR])
                fr = Ring(es, nc, p + "f", [128, 3, 16], F32, 2)

            def qk_chunk(ps, Bps, which, c, tok0):
                gcol = gq if which == "q" else gk
                dst = qT_s if which == "q" else kT_s
                sq, Bsq = sqr.get()
                S.op("act", "activation", (sq[:], ps[:], AF.Square), reads=[Bps], writes=[Bsq])
                ms, Bms = msr.get()
                S.op("pe", "matmul", (ms[:], onesm, sq[:]), dict(start=True, stop=True),
                     reads=[Bsq, B_c], writes=[Bms])
                rt, Brt = rtr.get()
                S.op("act", "activation", (rt[:], ms[:], AF.Sqrt), dict(scale=inv_d, bias=EPS),
                     reads=[Bms], writes=[Brt])
                S.op("dve", "reciprocal", (rt[:], rt[:]), reads=[Brt], writes=[Brt])
                qn, Bqn = qnr.get()
                S.op("dve", "scalar_tensor_tensor",
                     (qn[:], ps[:], prm[:, gcol:gcol + 1], rt[:], ALU.mult, ALU.mult),
                     reads=[Bps, Brt, B_prm], writes=[Bqn])
                S.dma("sp", dst[c * 128:(c + 1) * 128, tok0:tok0 + 512], qn[:], reads=[Bqn])

            def fm_block(wt, Bwt, hT, hTB, fn):
                for oc in range(4):
                    ps, Bps = acc.get()
                    for kc in range(16):
                        S.op("pe", "matmul", (ps[:], wt[:, kc, oc * 128:(oc + 1) * 128], hT[:, kc, :]),
                             dict(start=(kc == 0), stop=(kc == 15)),
                             reads=[Bwt] + hT_reads(hTB, kc), writes=[Bps])
                    fn(oc, ps, Bps)

            def v_block(wt, Bwt, hT, hTB, h0, blk0):
                vs, Bvs = vsr.get()
                for tb in range(4):
                    ps, Bps = acc.get()
                    for kc in range(16):
                        S.op("pe", "matmul", (ps[:], hT[:, kc, tb * 128:(tb + 1) * 128], wt[:, kc, :]),
                             dict(start=(kc == 0), stop=(kc == 15)),
                             reads=[Bwt, hTB[tb][kc // 8]], writes=[Bps])
                    S.op("act", "copy", (vs[:, :, tb, :], ps[:].rearrange("p (h e) -> p h e", h=4)),
                         reads=[Bps], writes=[Bvs])
                S.dma("sp", v_s[h0:h0 + 4, :, blk0:blk0 + 4, :].rearrange("h p b e -> p h b e"),
                      vs[:], reads=[Bvs])

            if l == 0:
                order = [2, 3, 4, 5, 0, 1, 6, 7, 8, 9, 10, 11]
            else:
                order = list(range(12))
            wst = Stream(S, wr, [w_tiles[0, cb] for _ in range(NT) for cb in order], 2)
            xst = Stream(S, xr, [x_src[b * 128:(b + 1) * 128, :] for b in range(NB)], 2)
            for ti in range(NT):
                tok0 = ti * 512
                hT, hTB = hTr[ti % 2]
                for tb in range(4):
                    xt, Bxt = xst.next()
                    norm_block(p, xt[:], [Bxt, BgB], gB, hbr, junk, Bjunk, smr, tpr, hT, hTB, tb)
                for cb in order:
                    wt, Bwt = wst.next()
                    if l == 0 and cb in (2, 3):
                        def fn(oc, ps, Bps, cb=cb):
                            c = (cb - 2) * 4 + oc
                            S.op("act", "copy", (gc[:, c, :], ps[:]), reads=[Bps], writes=[Bgc[c]])
                        fm_block(wt, Bwt, hT, hTB, fn)
                    elif l == 0 and cb in (4, 5):
                        def fn(oc, ps, Bps, cb=cb):
                            c = (cb - 4) * 4 + oc
                            S.op("dve", "tensor_tensor", (z[:, c, 2:514], ps[:], gc[:, c, :], ALU.mult),
                                 reads=[Bps, Bgc[c]], writes=[Bz[c]])
                        fm_block(wt, Bwt, hT, hTB, fn)
                    elif l == 0 and cb in (0, 1):
                        def fn(oc, ps, Bps, cb=cb, tok0=tok0):
                            c = cb * 4 + oc
                            t, Bt = ctr.get()
                            S.op("pool", "tensor_scalar",
                                 (t[:], z[:, c, 2:514], cw[:, 2, c:c + 1], 0.0, ALU.mult, ALU.add),
                                 reads=[Bz[c], B_prm], writes=[Bt])
                            S.op("dve", "scalar_tensor_tensor",
                                 (t[:], z[:, c, 1:513], cw[:, 1, c:c + 1], t[:], ALU.mult, ALU.add),
                                 reads=[Bz[c], B_prm, Bt], writes=[Bt])
                            S.op("dve", "scalar_tensor_tensor",
                                 (t[:], z[:, c, 0:512], cw[:, 0, c:c + 1], t[:], ALU.mult, ALU.add),
                                 reads=[Bz[c], B_prm, Bt], writes=[Bt])
                            y, By = yr.get()
                            S.op("dve", "tensor_tensor", (y[:], ps[:], t[:], ALU.mult),
                                 reads=[Bps, Bt], writes=[By])
                            S.dma("sp", aT_s[tok0 // 512][:, c, :], y[:], reads=[By])
                            S.op("pool", "tensor_copy", (z[:, c, 0:2], z[:, c, 512:514]),
                                 reads=[Bz[c]], writes=[Bz[c]])
                        fm_block(wt, Bwt, hT, hTB, fn)
                    elif (l == 0 and cb in (6, 7, 8, 9)) or (l == 1 and cb < 8):
                        if l == 0:
                            which = "q" if cb < 8 else "k"
                            c0 = (cb - 6) * 4 if cb < 8 else (cb - 8) * 4
                        else:
                            which = "q" if cb < 4 else "k"
                            c0 = cb * 4 if cb < 4 else (cb - 4) * 4

                        def fn(oc, ps, Bps, which=which, c0=c0, tok0=tok0):
                            qk_chunk(ps, Bps, which, c0 + oc, tok0)
                        fm_block(wt, Bwt, hT, hTB, fn)
                    else:
                        h0 = (cb - 10) * 4 if l == 0 else (cb - 8) * 4
                        v_block(wt, Bwt, hT, hTB, h0, ti * 4)
                if l == 1:
                    for tb in range(4):
                        blk = ti * 4 + tb
                        ps, Bps = acc.get()
                        for kc in range(16):
                            S.op("pe", "matmul", (ps[:, 0:16], hT[:, kc, tb * 128:(tb + 1) * 128], wf[:, kc, :]),
                                 dict(start=(kc == 0), stop=(kc == 15)),
                                 reads=[Bwf, hTB[tb][kc // 8]], writes=[Bps])
                        f, Bf = fr.get()
                        S.op("dve", "tensor_tensor", (f[:, 0, :], ps[:, 0:16], bfB[:], ALU.add),
                             reads=[Bps, B_prm], writes=[Bf])
                        S.op("act", "activation", (f[:, 1, :], f[:, 0, :], AF.Exp), dict(scale=-1.0),
                             reads=[Bf], writes=[Bf])
                        S.op("act", "activation", (f[:, 2, :], f[:, 1, :], AF.Ln), dict(bias=1.0),
                             reads=[Bf], writes=[Bf])
                        ps2, Bps2 = acc.get()
                        S.op("pe", "matmul", (ps2[:, 0:16], tri32, f[:, 2, :]), dict(start=True, stop=False),
                             reads=[Bf, B_c], writes=[Bps2])
                        S.op("pe", "matmul", (ps2[:, 0:16], ones32, R[:]), dict(start=False, stop=True),
                             reads=[BR, B_c], writes=[Bps2])
                        S.op("act", "copy", (LT[:, blk, :], ps2[:, 0:16]), reads=[Bps2], writes=[B_LT])
                        S.op("dve", "tensor_tensor", (R[:], R[:], f[:, 2, :], ALU.add),
                             reads=[BR, Bf], writes=[BR])
                        if tb == 3:
                            ps3, Bps3 = acc.get()
                            S.op("pe", "matmul", (ps3[:, 0:16], ones32, R[:]), dict(start=True, stop=True),
                                 reads=[BR, B_c], writes=[Bps3])
                            S.op("act", "copy", (Lref[:, ti, :], ps3[:, 0:16]), reads=[Bps3], writes=[B_Lref])
        S.barrier()

    def phase_B(l):
        with ExitStack() as es:
            p = f"B{l}"
            maps = 2 if l == 0 else 1
            nheads = 8 if l == 0 else 16
            QT = 512 // maps
            NQ = T // QT
            KB = QT // 128
            ktr = Ring(es, nc, p + "kt", [128, T], BF16, 2)
            if maps == 1:
                qtr = Ring(es, nc, p + "qt", [128, T], BF16, 2)
            else:
                qzr = [Ring(es, nc, p + f"qz{m}_", [128, T], BF16, 2) for m in range(2)]
                for m in range(2):
                    for (tz, bz) in qzr[m].items:
                        lo = 64 if m == 0 else 0
                        S.op("pool", "memset", (tz[lo:lo + 64, :], 0.0), writes=[bz])
            vr = Ring(es, nc, p + "v", [128, NB, 128], BF16, 2)
            Sr = Ring(es, nc, p + "S", [128, 512], F32, 3, psum=True)
            Or = Ring(es, nc, p + "O", [128, 512], F32, 2, psum=True)
            dr = Ring(es, nc, p + "dn", [128, 512], F32, 2, psum=True)
            msr = Ring(es, nc, p + "ms", [128, 512], F32, 1, psum=True)
            Pr = Ring(es, nc, p + "P", [128, 512], BF16, 4)
            rdr = Ring(es, nc, p + "rd", [128, 512], F32, 2)
            osr = Ring(es, nc, p + "os", [128, 512], BF16, 2)
            if l == 0:
                o12r = Ring(es, nc, p + "o12", [128, 512], F32, 2)
                ddr = Ring(es, nc, p + "dd", [128, 256], F32, 2)
                sq2r = Ring(es, nc, p + "sq2", [128, 256], BF16, 2)
                rt2r = Ring(es, nc, p + "rt2", [128, 256], F32, 2)
            else:
                bjr = Ring(es, nc, p + "bj", [128, NB], F32, 3)
                cnr = Ring(es, nc, p + "cn", [128, NB], F32, 2)
                rdr_ = Ring(es, nc, p + "rD", [128, T], BF16, 2)

            bg = None
            if l == 0 and bg_pieces:
                bgs = Ring(es, nc, p + "bgs", [128, 4, 512], F32, 3)
                bgb = Ring(es, nc, p + "bgb", [128, 4, 512], BF16, 3)
                bgst = Stream(S, bgs, [p_[1] for p_ in bg_pieces], 2)
                bg = {"k": 0}
                n_iter = nheads * sum((j + 1) * KB for j in range(NQ))
                bg_every = max(1, (n_iter * 3 // 4) // len(bg_pieces))

                def bg_step():
                    k = bg["k"]
                    if k >= len(bg_pieces):
                        return False
                    s_t, s_b = bgst.next()
                    b_t, b_b = bgb.get()
                    S.op("pool", "tensor_copy", (b_t[:], s_t[:]), reads=[s_b], writes=[b_b])
                    S.dma("sp", bg_pieces[k][0], b_t[:], reads=[b_b])
                    bg["k"] = k + 1
                    return True
            it_count = 0
            for h in range(nheads):
                kt, Bkt = ktr.get()
                vt, Bvt = vr.get()
                S.dma("sp", kt[:], kT_s[h * 128:(h + 1) * 128, :], writes=[Bkt])
                if maps == 1:
                    qt, Bqt = qtr.get()
                    S.dma("sp", qt[:], qT_s[h * 128:(h + 1) * 128, :], writes=[Bqt])
                else:
                    qz = [qzr[m].get() for m in range(2)]
                    for m in range(2):
                        S.dma("sp", qz[m][0][m * 64:(m + 1) * 64, :],
                              qT_s[h * 128 + m * 64:h * 128 + (m + 1) * 64, :], writes=[qz[m][1]])
                S.dma("sp", vt[:], v_s[h], writes=[Bvt])
                if l == 1:
                    cn, Bcn = cnr.get()
                    for jj in range(NQ):
                        S.op("dve", "tensor_scalar",
                             (cn[:, 4 * jj:4 * jj + 4], LT[:, 4 * jj:4 * jj + 4, h], Lref[:, jj, h:h + 1], -1.0,
                              ALU.subtract, ALU.mult), reads=[B_LT, B_Lref], writes=[Bcn])
                    rD, BrD = rdr_.get()
                    for bb in range(NB):
                        S.op("dve", "tensor_scalar",
                             (rD[:, bb * 128:(bb + 1) * 128], ident32, cn[:, bb:bb + 1], None, ALU.mult),
                             reads=[Bcn, B_c], writes=[BrD])
                seq = []
                for j in range(NQ):
                    nkb = (j + 1) * KB
                    for i in range(nkb):
                        seq.append((j, i, nkb))
                state = {}

                def issue_qk(idx):
                    j, i, nkb = seq[idx]
                    r = i - (nkb - KB)
                    c0 = max(r, 0) * 128
                    st, Bst = Sr.get()
                    if maps == 1:
                        S.op("pe", "matmul", (st[:, c0:512], kt[:, i * 128:(i + 1) * 128],
                                              qt[:, j * 512 + c0:(j + 1) * 512]),
                             dict(start=True, stop=False, skip_group_check=True), reads=[Bkt, Bqt], writes=[Bst])
                        S.op("pe", "matmul", (st[:, c0:512], ones_bf, rD[:, j * 512 + c0:(j + 1) * 512]),
                             dict(start=False, stop=(r < 0), skip_group_check=True), reads=[BrD, B_c], writes=[Bst])
                        if r >= 0:
                            S.op("pe", "matmul", (st[:, c0:c0 + 128], ident, trineg[:]),
                                 dict(start=False, stop=True, skip_group_check=True), reads=[B_c], writes=[Bst])
                    else:
                        for m in range(2):
                            S.op("pe", "matmul", (st[:, m * 256 + c0:(m + 1) * 256],
                                                  kt[:, i * 128:(i + 1) * 128],
                                                  qz[m][0][:, j * 256 + c0:(j + 1) * 256]),
                                 dict(start=True, stop=True, skip_group_check=True),
                                 reads=[Bkt, qz[m][1]], writes=[Bst])
                    state[idx] = (st, Bst, r, c0)

                LOOK = 2
                for idx in range(min(LOOK, len(seq))):
                    issue_qk(idx)
                cur = {}
                for idx in range(len(seq)):
                    j, i, nkb = seq[idx]
                    it_count += 1
                    if bg is not None and it_count % bg_every == 0:
                        bg_step()
                    if idx + LOOK < len(seq):
                        issue_qk(idx + LOOK)
                    st, Bst, r, c0 = state.pop(idx)
                    if i == 0:
                        ot, Bot = Or.get()
                        dn, Bdn = dr.get()
                        cur["o"] = (ot, Bot, dn, Bdn)
                        if l == 1:
                            def mk_bj(jj):
                                bj_, Bbj_ = bjr.get()
                                n_ = (jj + 1) * KB
                                S.op("dve", "tensor_scalar",
                                     (bj_[:, 0:n_], LT[:, 0:n_, h], Lref[:, jj, h:h + 1], None, ALU.subtract),
                                     reads=[B_LT, B_Lref], writes=[Bbj_])
                                return (bj_, Bbj_)
                            if j == 0:
                                cur["bj_next"] = mk_bj(0)
                            cur["bj"] = cur["bj_next"]
                            if j + 1 < NQ:
                                cur["bj_next"] = mk_bj(j + 1)
                    ot, Bot, dn, Bdn = cur["o"]
                    pt, Bpt = Pr.get()
                    if maps == 1:
                        bj, Bbj = cur["bj"]
                        S.op("act", "activation", (pt[:, c0:512], st[:, c0:512], AF.Exp),
                             dict(bias=bj[:, i:i + 1]), reads=[Bst, Bbj], writes=[Bpt])
                        S.op("pe", "matmul", (ot[:, c0:512], vt[:, i, :], pt[:, c0:512]),
                             dict(start=(i == 0), stop=(i == nkb - 1), skip_group_check=True),
                             reads=[Bvt, Bpt], writes=[Bot])
                        S.op("pe", "matmul", (dn[:, c0:512], ones_bf, pt[:, c0:512]),
                             dict(start=(i == 0), stop=(i == nkb - 1), skip_group_check=True),
                             reads=[Bpt, B_c], writes=[Bdn])
                    else:
                        st3 = st[:, :].rearrange("p (m q) -> p m q", m=2)
                        pt3 = pt[:, :].rearrange("p (m q) -> p m q", m=2)
                        S.op("act", "activation", (pt3[:, :, c0:256], st3[:, :, c0:256], AF.Exp),
                             reads=[Bst], writes=[Bpt])
                        if r >= 0:
                            for m in range(2):
                                a = m * 256 + c0
                                S.op("dve", "tensor_tensor", (pt[:, a:a + 128], pt[:, a:a + 128], tri_bf, ALU.mult),
                                     reads=[Bpt, B_c], writes=[Bpt])
                        for m in range(2):
                            a, b = m * 256 + c0, (m + 1) * 256
                            S.op("pe", "matmul", (ot[:, a:b], vt[:, i, :], pt[:, a:b]),
                                 dict(start=(i == 0 and m == 0), stop=(i == nkb - 1), skip_group_check=True),
                                 reads=[Bvt, Bpt], writes=[Bot])
                        if c0 == 0:
                            S.op("pe", "matmul", (dn[:, :], ones_bf, pt[:, :]),
                                 dict(start=(i == 0), stop=(i == nkb - 1), skip_group_check=True),
                                 reads=[Bpt, B_c], writes=[Bdn])
                        else:
                            for m in range(2):
                                a, b = m * 256 + c0, (m + 1) * 256
                                S.op("pe", "matmul", (dn[:, a:b], ones_bf, pt[:, a:b]),
                                     dict(start=False, stop=(i == nkb - 1), skip_group_check=True),
                                     reads=[Bpt, B_c], writes=[Bdn])
                    if i == nkb - 1:
                        rd, Brd = rdr.get()
                        S.op("dve", "reciprocal", (rd[:], dn[:]), reads=[Bdn], writes=[Brd])
                        if l == 1:
                            ost, Bos = osr.get()
                            S.op("dve", "tensor_tensor", (ost[:], ot[:], rd[:], ALU.mult),
                                 reads=[Bot, Brd], writes=[Bos])
                            S.dma("sp", aT_s[j][:, h, :], ost[:], reads=[Bos])
                        else:
                            o12, Bo12 = o12r.get()
                            S.op("dve", "tensor_tensor", (o12[:], ot[:], rd[:], ALU.mult),
                                 reads=[Bot, Brd], writes=[Bo12])
                            dd, Bdd = ddr.get()
                            S.op("dve", "scalar_tensor_tensor",
                                 (dd[:], o12[:, 256:512], prm[:, PNLAM:PNLAM + 1], o12[:, 0:256], ALU.mult, ALU.add),
                                 reads=[Bo12, B_prm], writes=[Bdd])
                            sq2, Bsq2 = sq2r.get()
                            S.op("act", "activation", (sq2[:], dd[:], AF.Square), reads=[Bdd], writes=[Bsq2])
                            ms, Bms = msr.get()
                            S.op("pe", "matmul", (ms[:, 0:256], ones_bf, sq2[:]), dict(start=True, stop=True),
                                 reads=[Bsq2, B_c], writes=[Bms])
                            rt2, Brt2 = rt2r.get()
                            S.op("act", "activation", (rt2[:], ms[:, 0:256], AF.Sqrt),
                                 dict(scale=1.0 / 128, bias=EPS), reads=[Bms], writes=[Brt2])
                            S.op("dve", "reciprocal", (rt2[:], rt2[:]), reads=[Brt2], writes=[Brt2])
                            ost, Bos = osr.get()
                            S.op("dve", "scalar_tensor_tensor",
                                 (ost[:, 0:256], dd[:], prm[:, PSUB:PSUB + 1], rt2[:], ALU.mult, ALU.mult),
                                 reads=[Bdd, Brt2, B_prm], writes=[Bos])
                            S.dma("sp", aT_s[j // 2][:, 8 + h, (j % 2) * 256:(j % 2 + 1) * 256],
                                  ost[:, 0:256], reads=[Bos])
            if bg is not None:
                while bg_step():
                    pass
        S.barrier()

    def phase_C(l, x_src, x_dst, w_out_t, Bout):
        with ExitStack() as es:
            p = f"C{l}"
            wr = Ring(es, nc, p + "w", [128, 16, 512], BF16, 3)
            uT = es.enter_context(nc.sbuf_tensor(p + "uT", [128, 64, 512], BF16))
            BuT = [Buf() for _ in range(64)]
            xt = es.enter_context(nc.sbuf_tensor(p + "xt", [128, 4, D], F32))
            Bx = [[Buf() for _ in range(4)] for _ in range(4)]
            xin = Ring(es, nc, p + "xin", [128, 4, 512], F32, 2)
            ah = es.enter_context(nc.sbuf_tensor(p + "ah", [128, 16, 512], BF16))
            ahB = [[Buf(), Buf()] for _ in range(4)]
            hbr = Ring(es, nc, p + "hb", [128, D], BF16, 1)
            junk = es.enter_context(nc.sbuf_tensor(p + "junk", [128, D], BF16))
            Bjunk = Buf()
            smr = Ring(es, nc, p + "sm", [128, 4], F32, 4)
            gB = es.enter_context(nc.sbuf_tensor(p + "g", [128, D], F32))
            BgB = Buf()
            S.dma("sp", gB[:], norm2_g[l:l + 1, :].partition_broadcast(128), writes=[BgB])
            rr = Ring(es, nc, p + "r", [128, 512], F32, 2)
            acc = Ring(es, nc, p + "acc", [128, 512], F32, 6, psum=True)
            tpr = Ring(es, nc, p + "tp", [128, 8, 128], BF16, 2, psum=True)
            all_ah = [b for tbb in ahB for b in tbb]
            wlist = []
            for ti in range(NT):
                wlist += [w_out_t[0, cb] for cb in range(4)]
                wlist += [wc_w1[l][0, cb] for cb in range(16)]
                wlist += [wc_w2[l][kg, cb] for cb in range(4) for kg in range(4)]
            wst = Stream(S, wr, wlist, 2)
            xist = Stream(S, xin, [x_src[ti * 512:(ti + 1) * 512, cb * 512:(cb + 1) * 512].rearrange(
                "(tb p) d -> p tb d", p=128) for ti in range(NT) for cb in range(4)], 1)
            S.dma("sp", ah[:], aT_s[0], writes=all_ah)
            for ti in range(NT):
                tok0 = ti * 512
                for cb in range(4):
                    wt, Bwt = wst.next()
                    xi, Bxi = xist.next()
                    for tb in range(4):
                        ps, Bps = acc.get()
                        for kc in range(16):
                            S.op("pe", "matmul", (ps[:], ah[:, kc, tb * 128:(tb + 1) * 128], wt[:, kc, :]),
                                 dict(start=(kc == 0), stop=(kc == 15)),
                                 reads=[Bwt, ahB[tb][kc // 8]], writes=[Bps])
                        S.op("dve", "tensor_tensor", (xt[:, tb, cb * 512:(cb + 1) * 512], ps[:], xi[:, tb, :], ALU.add),
                             reads=[Bps, Bxi], writes=[Bx[tb][cb]])
                for tb in range(4):
                    norm_block(p, xt[:, tb, :], Bx[tb] + [BgB], gB, hbr, junk, Bjunk, smr, tpr, ah, ahB, tb)
                for cb in range(16):
                    wt, Bwt = wst.next()
                    for oc in range(4):
                        ps, Bps = acc.get()
                        for kc in range(16):
                            S.op("pe", "matmul", (ps[:], wt[:, kc, oc * 128:(oc + 1) * 128], ah[:, kc, :]),
                                 dict(start=(kc == 0), stop=(kc == 15)),
                                 reads=[Bwt] + hT_reads(ahB, kc), writes=[Bps])
                        r, Br = rr.get()
                        S.op("act", "activation", (r[:], ps[:], AF.Relu), reads=[Bps], writes=[Br])
                        c = cb * 4 + oc
                        S.op("dve", "tensor_tensor", (uT[:, c, :], r[:], r[:], ALU.mult),
                             reads=[Br], writes=[BuT[c]])
                if ti + 1 < NT:
                    S.dma("sp", ah[:], aT_s[ti + 1], writes=all_ah)
                for cb in range(4):
                    pss = [acc.get() for _ in range(4)]
                    for kg in range(4):
                        wt, Bwt = wst.next()
                        for tb in range(4):
                            ps, Bps = pss[tb]
                            for kc in range(16):
                                c = kg * 16 + kc
                                S.op("pe", "matmul", (ps[:], uT[:, c, tb * 128:(tb + 1) * 128], wt[:, kc, :]),
                                     dict(start=(kg == 0 and kc == 0), stop=(kg == 3 and kc == 15)),
                                     reads=[Bwt, BuT[c]], writes=[Bps])
                    for tb in range(4):
                        ps, Bps = pss[tb]
                        sl = xt[:, tb, cb * 512:(cb + 1) * 512]
                        S.op("dve", "tensor_tensor", (sl, ps[:], sl, ALU.add),
                             reads=[Bps, Bx[tb][cb]], writes=[Bx[tb][cb]])
                    S.dma("sp", x_dst[tok0:tok0 + 512, cb * 512:(cb + 1) * 512].rearrange(
                        "(tb p) d -> p tb d", p=128), xt[:, :, cb * 512:(cb + 1) * 512],
                        reads=[Bx[tb][cb] for tb in range(4)], writes=[Bout])
        S.barrier()

    B_x1 = Buf("x1")
    B_out = Buf("out")
    if phases is None:
        phases = ["A0", "B0", "C0", "A1", "B1", "C1"]
    if "A0" in phases:
        phase_A(0, x_in)
    if "B0" in phases:
        phase_B(0)
    if "C0" in phases:
        phase_C(0, x_in, x1_s if "C1" in phases else out, wc_hout, B_x1 if "C1" in phases else B_out)
    if "A1" in phases:
        phase_A(1, x1_s)
    if "B1" in phases:
        phase_B(1)
    if "C1" in phases:
        phase_C(1, x1_s, out, wc_fout, B_out)
    S.finish([B_out])
    ges.close()
    return nc, S


def make_consts():
    c = np.zeros((128, 4, 128), np.float32)
    c[:, 0, :] = np.eye(128, dtype=np.float32)
    c[:, 1, :] = (np.arange(128)[None, :] >= np.arange(128)[:, None]).astype(np.float32)
    c[:, 2, :] = 1.0
    c[0:64, 3, 0:64] = 1.0
    c[64:128, 3, 64:128] = 1.0
    return c


PARAM_NAMES = ["norm1_g", "norm2_g", "hyb_w_in", "hyb_conv_w", "hyb_dq_g", "hyb_dk_g", "hyb_lq1", "hyb_lk1",
               "hyb_lq2", "hyb_lk2", "hyb_subln_g", "hyb_w_out", "fox_w_in", "fox_b_f", "fox_q_g", "fox_k_g",
               "fox_w_out", "mlp_w1", "mlp_w2"]


def kernel(**inputs):
    x = np.ascontiguousarray(inputs["x"], dtype=np.float32)
    B, T, _ = x.shape
    nc, _ = build(T)
    cst = make_consts()
    params = {k: np.ascontiguousarray(inputs[k], dtype=np.float32) for k in PARAM_NAMES}
    slots = [0, 1, None, None, 2, 3] if B == 4 else list(range(B))
    zeros = None
    in_maps = []
    for s in slots:
        if s is None:
            if zeros is None:
                zeros = {k: np.zeros_like(v) for k, v in params.items()}
                zeros["x"] = np.zeros_like(x[0])
                zeros["cst"] = np.zeros_like(cst)
            in_maps.append(zeros)
        else:
            m = dict(params)
            m["x"] = x[s]
            m["cst"] = cst
            in_maps.append(m)
    res = run_bass_kernel_spmd(nc, in_maps, core_ids=list(range(len(slots))))
    out = np.empty((B, T, D), np.float32)
    for c, s in enumerate(slots):
        if s is not None:
            out[s] = np.asarray(res.results[c]["out"])
    return out
```

```python
import math
from contextlib import ExitStack

import numpy as np
import concourse.bass as bass
import concourse.mybir as mybir
from concourse.bass_utils import run_bass_kernel_spmd

F32 = mybir.dt.float32
BF16 = mybir.dt.bfloat16
AF = mybir.ActivationFunctionType
ALU = mybir.AluOpType
AX = mybir.AxisListType

D = 2048
DFF = 8192
EPS = 1e-6
SEM_CAP = 30000
DMA_RING = 6


class Buf:
    __slots__ = ("name", "last_w", "readers")

    def __init__(self, name=""):
        self.name = name
        self.last_w = None
        self.readers = []


class Op:
    __slots__ = ("eng", "method", "args", "kw", "reads", "writes", "is_dma", "deps",
                 "event", "signal", "bar_last")

    def __init__(self, eng, method, args, kw, reads, writes, is_dma):
        self.eng = eng
        self.method = method
        self.args = args
        self.kw = kw
        self.reads = reads
        self.writes = writes
        self.is_dma = is_dma
        self.deps = None
        self.event = None
        self.signal = False
        self.bar_last = None


class Sched:
    ENGS = ("pe", "act", "dve", "pool", "sp")
    COMPUTE = ("pe", "act", "dve", "pool")

    def __init__(self, nc, same_engine_sync=True):
        self.nc = nc
        self.eng = {"pe": nc.tensor, "act": nc.scalar, "dve": nc.vector,
                    "pool": nc.gpsimd, "sp": nc.sync}
        self.ops = []
        self.same_engine_sync = same_engine_sync
        self.cur_sem = {}
        self.cur_cnt = {}
        self.waited = {e: {} for e in self.ENGS}
        self.dma_ring = {}
        self.dma_next = {}
        self.nsem = 0
        self.n_emitted = {e: 0 for e in self.ENGS}
        self.n_waits = {e: 0 for e in self.ENGS}

    def op(self, eng, method, args=(), kw=None, reads=(), writes=()):
        o = Op(eng, method, args, kw or {}, list(reads), list(writes), False)
        self.ops.append(o)
        return o

    def dma(self, eng, out_ap, in_ap, reads=(), writes=(), **kw):
        o = Op(eng, "dma_start", (), dict(out=out_ap, in_=in_ap, **kw), list(reads), list(writes), True)
        self.ops.append(o)
        return o

    def barrier(self):
        o = Op(None, "barrier", (), {}, [], [], False)
        self.ops.append(o)

    def _new_sem(self, name):
        self.nsem += 1
        return self.nc.alloc_semaphore(f"{name}_{self.nsem}")

    def _eng_event(self, e):
        if e not in self.cur_sem or self.cur_cnt[e] >= SEM_CAP:
            self.cur_sem[e] = self._new_sem(f"s_{e}")
            self.cur_cnt[e] = 0
        self.cur_cnt[e] += 1
        return (self.cur_sem[e], self.cur_cnt[e])

    def _wait(self, e, sem, val):
        w = self.waited[e]
        k = id(sem)
        if w.get(k, (None, 0))[1] >= val:
            return
        self.eng[e].wait_ge(sem, val)
        self.n_waits[e] += 1
        w[k] = (sem, val)

    def flush(self):
        ops = self.ops
        self.ops = []
        last = {}
        for o in ops:
            if o.method == "barrier":
                o.bar_last = dict(last)
                for d in last.values():
                    d.signal = True
                continue
            deps = []
            for b in o.reads:
                if b.last_w is not None:
                    deps.append(b.last_w)
            for b in o.writes:
                if b.last_w is not None:
                    deps.append(b.last_w)
                deps.extend(b.readers)
            o.deps = deps
            for b in o.reads:
                b.readers.append(o)
            for b in o.writes:
                b.last_w = o
                b.readers = []
            for d in deps:
                if d.is_dma:
                    continue
                if d.eng != o.eng or (self.same_engine_sync and d.eng != "pe"):
                    d.signal = True
            if not o.is_dma:
                last[o.eng] = o
        for o in ops:
            if o.method == "barrier":
                evs = [d.event for d in o.bar_last.values()]
                for ring in self.dma_ring.values():
                    for slot in ring:
                        if slot[1] > 0:
                            evs.append((slot[0], slot[1]))
                for e in self.ENGS:
                    for sem, val in evs:
                        self._wait(e, sem, val)
                continue
            e = o.eng
            engine = self.eng[e]
            waits = {}
            for d in o.deps:
                if d.event is None:
                    continue
                if (not d.is_dma) and d.eng == e and (e == "pe" or not self.same_engine_sync):
                    continue
                sem, val = d.event
                k = id(sem)
                if waits.get(k, (None, 0))[1] < val:
                    waits[k] = (sem, val)
            if o.is_dma:
                if e not in self.dma_ring:
                    self.dma_ring[e] = [[self._new_sem(f"d_{e}"), 0] for _ in range(DMA_RING)]
                ring = self.dma_ring[e]
                j = self.dma_next.get(e, 0)
                self.dma_next[e] = j + 1
                slot = ring[j % DMA_RING]
                if slot[1] > 0:
                    k = id(slot[0])
                    if waits.get(k, (None, 0))[1] < slot[1]:
                        waits[k] = (slot[0], slot[1])
            for sem, val in waits.values():
                self._wait(e, sem, val)
            ins = getattr(engine, o.method)(*o.args, **o.kw)
            self.n_emitted[e] += 1
            if o.is_dma:
                slot[1] += 16
                ins.then_inc(slot[0], 16)
                o.event = (slot[0], slot[1])
            elif o.signal:
                ev = self._eng_event(e)
                ins.then_inc(ev[0], 1)
                o.event = ev
            o.deps = None
            o.args = None
            o.kw = None

    def finish(self, final_bufs):
        self.flush()
        sp = self.eng["sp"]
        for b in final_bufs:
            o = b.last_w
            if o is None:
                continue
            sem, val = o.event
            sp.wait_ge(sem, val)


class Ring:
    def __init__(self, es, nc, name, shape, dt, n, psum=False):
        self.items = []
        for i in range(n):
            mk = nc.psum_tensor if psum else nc.sbuf_tensor
            t = es.enter_context(mk(f"{name}{i}", shape, dt))
            self.items.append((t, Buf(f"{name}{i}")))
        self.k = 0

    def get(self):
        it = self.items[self.k % len(self.items)]
        self.k += 1
        return it


class Stream:
    def __init__(self, S, ring, srcs, depth, bufs_fn=None):
        self.S, self.ring, self.srcs, self.depth = S, ring, srcs, depth
        self.issued = []
        self.k = 0

    def _issue(self):
        i = len(self.issued)
        t, b = self.ring.get()
        self.S.dma("sp", t[:], self.srcs[i], writes=[b])
        self.issued.append((t, b))

    def next(self):
        while len(self.issued) < min(len(self.srcs), self.k + 1 + self.depth):
            self._issue()
        it = self.issued[self.k]
        self.k += 1
        return it


WSPECS = [
    ("hyb_w_in", D, 6144), ("hyb_w_out", D, D), ("fox_w_in", D, 6160), ("fox_w_out", D, D),
]


def build(T, dbg=False, phases=None):
    nc = bass.Bass("TRN2", target_bir_lowering=False)
    NT = T // 512
    NB = T // 128

    def din(name, shape):
        return nc.dram_tensor(name, shape, F32, kind="ExternalInput").ap()

    x_in = din("x", [T, D])
    norm1_g = din("norm1_g", [2, D])
    norm2_g = din("norm2_g", [2, D])
    hyb_w_in = din("hyb_w_in", [1, D, 6144])
    hyb_conv_w = din("hyb_conv_w", [1, 3, 1024])
    hyb_dq_g = din("hyb_dq_g", [1, 64])
    hyb_dk_g = din("hyb_dk_g", [1, 64])
    hyb_lq1 = din("hyb_lq1", [1, 64])
    hyb_lk1 = din("hyb_lk1", [1, 64])
    hyb_lq2 = din("hyb_lq2", [1, 64])
    hyb_lk2 = din("hyb_lk2", [1, 64])
    hyb_subln_g = din("hyb_subln_g", [1, 128])
    hyb_w_out = din("hyb_w_out", [1, D, D])
    fox_w_in = din("fox_w_in", [1, D, 6160])
    fox_b_f = din("fox_b_f", [1, 16])
    fox_q_g = din("fox_q_g", [1, 128])
    fox_k_g = din("fox_k_g", [1, 128])
    fox_w_out = din("fox_w_out", [1, D, D])
    mlp_w1 = din("mlp_w1", [2, D, DFF])
    mlp_w2 = din("mlp_w2", [2, DFF, D])
    cst = din("cst", [128, 4, 128])
    out = nc.dram_tensor("out", [T, D], F32, kind="ExternalOutput").ap()

    def scr(name, shape, dt=BF16):
        return nc.dram_tensor(name, shape, dt).ap()

    wc_hin = scr("wc_hin", [1, 12, 128, 16, 512])
    wc_hout = scr("wc_hout", [1, 4, 128, 16, 512])
    wc_fin = scr("wc_fin", [1, 12, 128, 16, 512])
    wc_ff = scr("wc_ff", [128, 16, 16])
    wc_fout = scr("wc_fout", [1, 4, 128, 16, 512])
    wc_w1 = [scr(f"wc_w1_{l}", [1, 16, 128, 16, 512]) for l in range(2)]
    wc_w2 = [scr(f"wc_w2_{l}", [4, 4, 128, 16, 512]) for l in range(2)]
    qT_s = scr("qT_s", [2048, T])
    kT_s = scr("kT_s", [2048, T])
    v_s = scr("v_s", [16, 128, NB, 128])
    aT_s = scr("aT_s", [NT, 128, 16, 512])
    x1_s = scr("x1_s", [T, D], F32)
    dbg_out = {}

    S = Sched(nc)
    ges = ExitStack()

    def gsb(name, shape, dt):
        return ges.enter_context(nc.sbuf_tensor(name, shape, dt))

    c_bf = gsb("c_bf", [128, 4, 128], BF16)
    c_f32 = gsb("c_f32", [128, 4, 128], F32)
    B_c = Buf("consts")
    S.dma("sp", c_f32[:], cst[:, :, :], writes=[B_c])
    S.op("dve", "tensor_copy", (c_bf[:], c_f32[:]), reads=[B_c], writes=[B_c])
    trineg = gsb("trineg", [128, 128], BF16)
    S.op("dve", "tensor_scalar", (trineg[:], c_f32[:, 1, :], -1.0, 30000.0, ALU.add, ALU.mult),
         reads=[B_c], writes=[B_c])
    ident = c_bf[:, 0, :]
    tri_bf = c_bf[:, 1, :]
    ones_bf = c_bf[:, 2, :]
    blk64_bf = c_bf[:, 3, :]
    ident32 = c_f32[:, 0, :]
    tri32 = c_f32[:, 1, :]
    ones32 = c_f32[:, 2, :]
    LT = gsb("LT", [128, NB, 16], F32)
    Lref = gsb("Lref", [128, NT, 16], F32)
    B_LT = Buf("LT")
    B_Lref = Buf("Lref")
    prm = gsb("prm", [128, 16], F32)
    B_prm = Buf("prm")
    PQ0, PK0, PSUB, PQ1, PK1, PNLAM = range(6)

    def col_load(dst_ap, src_row_ap, n):
        S.dma("sp", dst_ap, src_row_ap.rearrange("o d -> d o"), writes=[B_prm])

    col_load(prm[0:64, PQ0:PQ0 + 1], hyb_dq_g[0:1, :], 64)
    col_load(prm[64:128, PQ0:PQ0 + 1], hyb_dq_g[0:1, :], 64)
    col_load(prm[0:64, PK0:PK0 + 1], hyb_dk_g[0:1, :], 64)
    col_load(prm[64:128, PK0:PK0 + 1], hyb_dk_g[0:1, :], 64)
    col_load(prm[:, PSUB:PSUB + 1], hyb_subln_g[0:1, :], 128)
    col_load(prm[:, PQ1:PQ1 + 1], fox_q_g[0:1, :], 128)
    col_load(prm[:, PK1:PK1 + 1], fox_k_g[0:1, :], 128)
    lam_init = 0.8 - 0.6 * math.exp(-0.3 * 0)
    S.op("dve", "tensor_scalar", (prm[:, PQ0:PQ0 + 1], prm[:, PQ0:PQ0 + 1], 64 ** -0.5, None, ALU.mult),
         reads=[B_prm], writes=[B_prm])
    S.op("dve", "tensor_scalar", (prm[:, PSUB:PSUB + 1], prm[:, PSUB:PSUB + 1], 1.0 - lam_init, None, ALU.mult),
         reads=[B_prm], writes=[B_prm])
    S.op("dve", "tensor_scalar", (prm[:, PQ1:PQ1 + 1], prm[:, PQ1:PQ1 + 1], 128 ** -0.5, None, ALU.mult),
         reads=[B_prm], writes=[B_prm])
    lv = gsb("lv", [128, 4, 64], F32)
    lw = gsb("lw", [128, 2, 64], F32)
    le = gsb("le", [128, 4], F32)
    B_l = Buf("lam")
    for i, a in enumerate((hyb_lq1, hyb_lk1, hyb_lq2, hyb_lk2)):
        S.dma("sp", lv[:, i, :], a[0:1, :].partition_broadcast(128), writes=[B_l])
    S.op("dve", "tensor_tensor", (lw[:, 0, :], lv[:, 0, :], lv[:, 1, :], ALU.mult), reads=[B_l], writes=[B_l])
    S.op("dve", "tensor_tensor", (lw[:, 1, :], lv[:, 2, :], lv[:, 3, :], ALU.mult), reads=[B_l], writes=[B_l])
    S.op("dve", "tensor_reduce", (le[:, 0:2], lw[:, :, :], AX.X, ALU.add), reads=[B_l], writes=[B_l])
    S.op("act", "activation", (le[:, 2:4], le[:, 0:2], AF.Exp), reads=[B_l], writes=[B_l])
    S.op("dve", "tensor_tensor", (le[:, 0:1], le[:, 3:4], le[:, 2:3], ALU.subtract), reads=[B_l], writes=[B_l])
    S.op("dve", "tensor_scalar", (prm[:, PNLAM:PNLAM + 1], le[:, 0:1], -lam_init, None, ALU.add),
         reads=[B_l, B_prm], writes=[B_prm])
    bfB = gsb("bfB", [128, 16], F32)
    S.dma("sp", bfB[:], fox_b_f[0:1, :].partition_broadcast(128), writes=[B_prm])
    cw = gsb("cw", [128, 3, 8], F32)
    cwr = gsb("cwr", [24, 128], F32)
    B_cwr = Buf()
    S.dma("sp", cwr[:], hyb_conv_w[0].rearrange("k (c p) -> (k c) p", p=128), writes=[B_cwr])
    with nc.psum_tensor("cw_ps", [128, 24], F32) as cw_ps:
        B_cwps = Buf()
        S.op("pe", "transpose", (cw_ps[:], cwr[:], c_f32[0:24, 0, 0:24]), reads=[B_cwr, B_c], writes=[B_cwps])
        S.op("dve", "tensor_copy", (cw[:].rearrange("p k c -> p (k c)"), cw_ps[:]), reads=[B_cwps], writes=[B_prm])
        S.flush()

    bg_pieces = []
    with ExitStack() as es:
        pieces = []

        def cast_w(dst, src, K, N):
            for kg in range(K // 2048):
                for cb in range(N // 512):
                    for q in range(4):
                        r0 = kg * 2048 + q * 512
                        pieces.append((dst[kg, cb][:, q * 4:(q + 1) * 4, :],
                                       src[r0:r0 + 512, cb * 512:(cb + 1) * 512].rearrange(
                                           "(kc p) c -> p kc c", p=128)))

        cast_w(wc_hin, hyb_w_in[0], D, 6144)
        n_now = len(pieces)
        cast_w(wc_hout, hyb_w_out[0], D, D)
        cast_w(wc_w1[0], mlp_w1[0], D, DFF)
        cast_w(wc_w2[0], mlp_w2[0], DFF, D)
        cast_w(wc_fin, fox_w_in[0], D, 6144)
        cast_w(wc_fout, fox_w_out[0], D, D)
        cast_w(wc_w1[1], mlp_w1[1], D, DFF)
        cast_w(wc_w2[1], mlp_w2[1], DFF, D)
        bg_pieces.extend(pieces[n_now:])
        pieces = pieces[:n_now]
        str_ = Ring(es, nc, "pp_s", [128, 4, 512], F32, 4)
        btr = Ring(es, nc, "pp_b", [128, 4, 512], BF16, 4)
        st = Stream(S, str_, [p_[1] for p_ in pieces], 3)
        engs = [("dve", "tensor_copy"), ("act", "copy"), ("pool", "tensor_copy")]
        for i, (dst, _) in enumerate(pieces):
            s_t, s_b = st.next()
            b_t, b_b = btr.get()
            e, m = engs[i % 3]
            S.op(e, m, (b_t[:], s_t[:]), reads=[s_b], writes=[b_b])
            S.dma("sp", dst, b_t[:], reads=[b_b])
        ffs = es.enter_context(nc.sbuf_tensor("pp_ffs", [128, 16, 16], F32))
        ffb = es.enter_context(nc.sbuf_tensor("pp_ffb", [128, 16, 16], BF16))
        B_ff = Buf()
        for kc in range(16):
            S.dma("sp", ffs[:, kc, :], fox_w_in[0][kc * 128:(kc + 1) * 128, 6144:6160], writes=[B_ff])
        S.op("dve", "tensor_copy", (ffb[:], ffs[:]), reads=[B_ff], writes=[B_ff])
        S.dma("sp", wc_ff[:, :, :], ffb[:], reads=[B_ff])
        S.barrier()
        S.flush()

    def norm_block(es_tag, xt_ap, xt_bufs, gB, hb_ring, junk, Bjunk, sm_ring, tp_ring, hT, hTB, tb):
        sm, Bsm = sm_ring.get()
        S.op("act", "activation", (junk[:], xt_ap, AF.Square), dict(accum_out=sm[:, 0:1]),
             reads=xt_bufs, writes=[Bjunk, Bsm])
        S.op("act", "activation", (sm[:, 1:2], sm[:, 0:1], AF.Sqrt), dict(scale=1.0 / D, bias=EPS),
             reads=[Bsm], writes=[Bsm])
        S.op("dve", "reciprocal", (sm[:, 2:3], sm[:, 1:2]), reads=[Bsm], writes=[Bsm])
        hb, Bhb = hb_ring.get()
        S.op("dve", "scalar_tensor_tensor", (hb[:], xt_ap, sm[:, 2:3], gB[:], ALU.mult, ALU.mult),
             reads=xt_bufs + [Bsm], writes=[Bhb])
        for half in range(2):
            pt, Bpt = tp_ring.get()
            for k in range(8):
                kc = half * 8 + k
                S.op("pe", "transpose", (pt[:, k, :], hb[:, kc * 128:(kc + 1) * 128], ident),
                     reads=[Bhb], writes=[Bpt])
            dst = hT[:, half * 8:(half + 1) * 8, tb * 128:(tb + 1) * 128]
            if half == 0:
                S.op("act", "copy", (dst, pt[:, :, :]), reads=[Bpt], writes=[hTB[tb][half]])
            else:
                S.op("dve", "tensor_copy", (dst, pt[:, :, :]), reads=[Bpt], writes=[hTB[tb][half]])

    def hT_reads(hTB, kc):
        return [hTB[tb][kc // 8] for tb in range(4)]

    def phase_A(l, x_src):
        with ExitStack() as es:
            p = f"A{l}"
            xr = Ring(es, nc, p + "x", [128, D], F32, 3)
            hbr = Ring(es, nc, p + "hb", [128, D], BF16, 2)
            junk = es.enter_context(nc.sbuf_tensor(p + "junk", [128, D], BF16))
            Bjunk = Buf()
            smr = Ring(es, nc, p + "sm", [128, 4], F32, 4)
            hTr = [(es.enter_context(nc.sbuf_tensor(f"{p}hT{i}", [128, 16, 512], BF16)),
                    [[Buf(), Buf()] for _ in range(4)]) for i in range(2)]
            wr = Ring(es, nc, p + "w", [128, 16, 512], BF16, 3)
            gB = es.enter_context(nc.sbuf_tensor(p + "g", [128, D], F32))
            BgB = Buf()
            S.dma("sp", gB[:], norm1_g[l:l + 1, :].partition_broadcast(128), writes=[BgB])
            acc = Ring(es, nc, p + "acc", [128, 512], F32, 5, psum=True)
            tpr = Ring(es, nc, p + "tp", [128, 8, 128], BF16, 2, psum=True)
            msr = Ring(es, nc, p + "ms", [128, 512], F32, 1, psum=True)
            sqr = Ring(es, nc, p + "sq", [128, 512], BF16, 2)
            rtr = Ring(es, nc, p + "rt", [128, 512], F32, 2)
            qnr = Ring(es, nc, p + "qn", [128, 512], BF16, 3)
            vsr = Ring(es, nc, p + "vs", [128, 4, 4, 128], BF16, 2)
            if l == 0:
                gc = es.enter_context(nc.sbuf_tensor(p + "gc", [128, 8, 512], BF16))
                Bgc = [Buf() for _ in range(8)]
                z = es.enter_context(nc.sbuf_tensor(p + "z", [128, 8, 516], F32))
                Bz = [Buf() for _ in range(8)]
                S.op("pool", "memset", (z[:, :, 0:2], 0.0), writes=Bz)
                ctr = Ring(es, nc, p + "ct", [128, 512], F32, 2)
                yr = Ring(es, nc, p + "y", [128, 512], BF16, 3)
                w_tiles = wc_hin
                gq, gk, onesm, inv_d = PQ0, PK0, blk64_bf, 1.0 / 64
            else:
                w_tiles = wc_fin
                gq, gk, onesm, inv_d = PQ1, PK1, ones_bf, 1.0 / 128
                wf = es.enter_context(nc.sbuf_tensor(p + "wf", [128, 16, 16], BF16))
                Bwf = Buf()
                S.dma("sp", wf[:], wc_ff[:, :, :], writes=[Bwf])
                R = es.enter_context(nc.sbuf_tensor(p + "R", [128, 16], F32))
                BR = Buf()
                S.op("pool", "memset", (R[:], 0.0), writes=[BR])
                fr = Ring(es, nc, p + "f", [128, 3, 16], F32, 2)

            def qk_chunk(ps, Bps, which, c, tok0):
                gcol = gq if which == "q" else gk
                dst = qT_s if which == "q" else kT_s
                sq, Bsq = sqr.get()
                S.op("act", "activation", (sq[:], ps[:], AF.Square), reads=[Bps], writes=[Bsq])
                ms, Bms = msr.get()
                S.op("pe", "matmul", (ms[:], onesm, sq[:]), dict(start=True, stop=True),
                     reads=[Bsq, B_c], writes=[Bms])
                rt, Brt = rtr.get()
                S.op("act", "activation", (rt[:], ms[:], AF.Sqrt), dict(scale=inv_d, bias=EPS),
                     reads=[Bms], writes=[Brt])
                S.op("dve", "reciprocal", (rt[:], rt[:]), reads=[Brt], writes=[Brt])
                qn, Bqn = qnr.get()
                S.op("dve", "scalar_tensor_tensor",
                     (qn[:], ps[:], prm[:, gcol:gcol + 1], rt[:], ALU.mult, ALU.mult),
                     reads=[Bps, Brt, B_prm], writes=[Bqn])
                S.dma("sp", dst[c * 128:(c + 1) * 128, tok0:tok0 + 512], qn[:], reads=[Bqn])

            def fm_block(wt, Bwt, hT, hTB, fn):
                for oc in range(4):
                    ps, Bps = acc.get()
                    for kc in range(16):
                        S.op("pe", "matmul", (ps[:], wt[:, kc, oc * 128:(oc + 1) * 128], hT[:, kc, :]),
                             dict(start=(kc == 0), stop=(kc == 15)),
                             reads=[Bwt] + hT_reads(hTB, kc), writes=[Bps])
                    fn(oc, ps, Bps)

            def v_block(wt, Bwt, hT, hTB, h0, blk0):
                vs, Bvs = vsr.get()
                for tb in range(4):
                    ps, Bps = acc.get()
                    for kc in range(16):
                        S.op("pe", "matmul", (ps[:], hT[:, kc, tb * 128:(tb + 1) * 128], wt[:, kc, :]),
                             dict(start=(kc == 0), stop=(kc == 15)),
                             reads=[Bwt, hTB[tb][kc // 8]], writes=[Bps])
                    S.op("act", "copy", (vs[:, :, tb, :], ps[:].rearrange("p (h e) -> p h e", h=4)),
                         reads=[Bps], writes=[Bvs])
                S.dma("sp", v_s[h0:h0 + 4, :, blk0:blk0 + 4, :].rearrange("h p b e -> p h b e"),
                      vs[:], reads=[Bvs])

            if l == 0:
                order = [2, 3, 4, 5, 0, 1, 6, 7, 8, 9, 10, 11]
            else:
                order = list(range(12))
            wst = Stream(S, wr, [w_tiles[0, cb] for _ in range(NT) for cb in order], 2)
            xst = Stream(S, xr, [x_src[b * 128:(b + 1) * 128, :] for b in range(NB)], 2)
            for ti in range(NT):
                tok0 = ti * 512
                hT, hTB = hTr[ti % 2]
                for tb in range(4):
                    xt, Bxt = xst.next()
                    norm_block(p, xt[:], [Bxt, BgB], gB, hbr, junk, Bjunk, smr, tpr, hT, hTB, tb)
                for cb in order:
                    wt, Bwt = wst.next()
                    if l == 0 and cb in (2, 3):
                        def fn(oc, ps, Bps, cb=cb):
                            c = (cb - 2) * 4 + oc
                            S.op("act", "copy", (gc[:, c, :], ps[:]), reads=[Bps], writes=[Bgc[c]])
                        fm_block(wt, Bwt, hT, hTB, fn)
                    elif l == 0 and cb in (4, 5):
                        def fn(oc, ps, Bps, cb=cb):
                            c = (cb - 4) * 4 + oc
                            S.op("dve", "tensor_tensor", (z[:, c, 2:514], ps[:], gc[:, c, :], ALU.mult),
                                 reads=[Bps, Bgc[c]], writes=[Bz[c]])
                        fm_block(wt, Bwt, hT, hTB, fn)
                    elif l == 0 and cb in (0, 1):
                        def fn(oc, ps, Bps, cb=cb, tok0=tok0):
                            c = cb * 4 + oc
                            t, Bt = ctr.get()
                            S.op("pool", "tensor_scalar",
                                 (t[:], z[:, c, 2:514], cw[:, 2, c:c + 1], 0.0, ALU.mult, ALU.add),
                                 reads=[Bz[c], B_prm], writes=[Bt])
                            S.op("dve", "scalar_tensor_tensor",
                                 (t[:], z[:, c, 1:513], cw[:, 1, c:c + 1], t[:], ALU.mult, ALU.add),
                                 reads=[Bz[c], B_prm, Bt], writes=[Bt])
                            S.op("dve", "scalar_tensor_tensor",
                                 (t[:], z[:, c, 0:512], cw[:, 0, c:c + 1], t[:], ALU.mult, ALU.add),
                                 reads=[Bz[c], B_prm, Bt], writes=[Bt])
                            y, By = yr.get()
                            S.op("dve", "tensor_tensor", (y[:], ps[:], t[:], ALU.mult),
                                 reads=[Bps, Bt], writes=[By])
                            S.dma("sp", aT_s[tok0 // 512][:, c, :], y[:], reads=[By])
                            S.op("pool", "tensor_copy", (z[:, c, 0:2], z[:, c, 512:514]),
                                 reads=[Bz[c]], writes=[Bz[c]])
                        fm_block(wt, Bwt, hT, hTB, fn)
                    elif (l == 0 and cb in (6, 7, 8, 9)) or (l == 1 and cb < 8):
                        if l == 0:
                            which = "q" if cb < 8 else "k"
                            c0 = (cb - 6) * 4 if cb < 8 else (cb - 8) * 4
                        else:
                            which = "q" if cb < 4 else "k"
                            c0 = cb * 4 if cb < 4 else (cb - 4) * 4

                        def fn(oc, ps, Bps, which=which, c0=c0, tok0=tok0):
                            qk_chunk(ps, Bps, which, c0 + oc, tok0)
                        fm_block(wt, Bwt, hT, hTB, fn)
                    else:
                        h0 = (cb - 10) * 4 if l == 0 else (cb - 8) * 4
                        v_block(wt, Bwt, hT, hTB, h0, ti * 4)
                if l == 1:
                    for tb in range(4):
                        blk = ti * 4 + tb
                        ps, Bps = acc.get()
                        for kc in range(16):
                            S.op("pe", "matmul", (ps[:, 0:16], hT[:, kc, tb * 128:(tb + 1) * 128], wf[:, kc, :]),
                                 dict(start=(kc == 0), stop=(kc == 15)),
                                 reads=[Bwf, hTB[tb][kc // 8]], writes=[Bps])
                        f, Bf = fr.get()
                        S.op("dve", "tensor_tensor", (f[:, 0, :], ps[:, 0:16], bfB[:], ALU.add),
                             reads=[Bps, B_prm], writes=[Bf])
                        S.op("act", "activation", (f[:, 1, :], f[:, 0, :], AF.Exp), dict(scale=-1.0),
                             reads=[Bf], writes=[Bf])
                        S.op("act", "activation", (f[:, 2, :], f[:, 1, :], AF.Ln), dict(bias=1.0),
                             reads=[Bf], writes=[Bf])
                        ps2, Bps2 = acc.get()
                        S.op("pe", "matmul", (ps2[:, 0:16], tri32, f[:, 2, :]), dict(start=True, stop=False),
                             reads=[Bf, B_c], writes=[Bps2])
                        S.op("pe", "matmul", (ps2[:, 0:16], ones32, R[:]), dict(start=False, stop=True),
                             reads=[BR, B_c], writes=[Bps2])
                        S.op("act", "copy", (LT[:, blk, :], ps2[:, 0:16]), reads=[Bps2], writes=[B_LT])
                        S.op("dve", "tensor_tensor", (R[:], R[:], f[:, 2, :], ALU.add),
                             reads=[BR, Bf], writes=[BR])
                        if tb == 3:
                            ps3, Bps3 = acc.get()
                            S.op("pe", "matmul", (ps3[:, 0:16], ones32, R[:]), dict(start=True, stop=True),
                                 reads=[BR, B_c], writes=[Bps3])
                            S.op("act", "copy", (Lref[:, ti, :], ps3[:, 0:16]), reads=[Bps3], writes=[B_Lref])
        S.barrier()

    def phase_B(l):
        with ExitStack() as es:
            p = f"B{l}"
            maps = 2 if l == 0 else 1
            nheads = 8 if l == 0 else 16
            QT = 512 // maps
            NQ = T // QT
            KB = QT // 128
            ktr = Ring(es, nc, p + "kt", [128, T], BF16, 2)
            if maps == 1:
                qtr = Ring(es, nc, p + "qt", [128, T], BF16, 2)
            else:
                qzr = [Ring(es, nc, p + f"qz{m}_", [128, T], BF16, 2) for m in range(2)]
                for m in range(2):
                    for (tz, bz) in qzr[m].items:
                        lo = 64 if m == 0 else 0
                        S.op("pool", "memset", (tz[lo:lo + 64, :], 0.0), writes=[bz])
            vr = Ring(es, nc, p + "v", [128, NB, 128], BF16, 2)
            Sr = Ring(es, nc, p + "S", [128, 512], F32, 3, psum=True)
            Or = Ring(es, nc, p + "O", [128, 512], F32, 2, psum=True)
            dr = Ring(es, nc, p + "dn", [128, 512], F32, 2, psum=True)
            msr = Ring(es, nc, p + "ms", [128, 512], F32, 1, psum=True)
            Pr = Ring(es, nc, p + "P", [128, 512], BF16, 4)
            rdr = Ring(es, nc, p + "rd", [128, 512], F32, 2)
            osr = Ring(es, nc, p + "os", [128, 512], BF16, 8)
            if l == 0:
                o12r = Ring(es, nc, p + "o12", [128, 512], F32, 2)
                ddr = Ring(es, nc, p + "dd", [128, 256], F32, 2)
                sq2r = Ring(es, nc, p + "sq2", [128, 256], BF16, 2)
                rt2r = Ring(es, nc, p + "rt2", [128, 256], F32, 2)
            else:
                bjr = Ring(es, nc, p + "bj", [128, NB], F32, 3)
                cnr = Ring(es, nc, p + "cn", [128, NB], F32, 2)
                rdr_ = Ring(es, nc, p + "rD", [128, T], BF16, 2)

            bg = None
            if l == 0 and bg_pieces:
                bgs = Ring(es, nc, p + "bgs", [128, 4, 512], F32, 3)
                bgb = Ring(es, nc, p + "bgb", [128, 4, 512], BF16, 3)
                bgst = Stream(S, bgs, [p_[1] for p_ in bg_pieces], 2)
                bg = {"k": 0}
                n_iter = nheads * sum((j + 1) * KB for j in range(NQ))
                bg_every = max(1, (n_iter * 3 // 4) // len(bg_pieces))

                def bg_step():
                    k = bg["k"]
                    if k >= len(bg_pieces):
                        return False
                    s_t, s_b = bgst.next()
                    b_t, b_b = bgb.get()
                    S.op("pool", "tensor_copy", (b_t[:], s_t[:]), reads=[s_b], writes=[b_b])
                    S.dma("sp", bg_pieces[k][0], b_t[:], reads=[b_b])
                    bg["k"] = k + 1
                    return True
            it_count = 0
            for h in range(nheads):
                kt, Bkt = ktr.get()
                vt, Bvt = vr.get()
                S.dma("sp", kt[:], kT_s[h * 128:(h + 1) * 128, :], writes=[Bkt])
                if maps == 1:
                    qt, Bqt = qtr.get()
                    S.dma("sp", qt[:], qT_s[h * 128:(h + 1) * 128, :], writes=[Bqt])
                else:
                    qz = [qzr[m].get() for m in range(2)]
                    for m in range(2):
                        S.dma("sp", qz[m][0][m * 64:(m + 1) * 64, :],
                              qT_s[h * 128 + m * 64:h * 128 + (m + 1) * 64, :], writes=[qz[m][1]])
                S.dma("sp", vt[:], v_s[h], writes=[Bvt])
                if l == 1:
                    cn, Bcn = cnr.get()
                    for jj in range(NQ):
                        S.op("dve", "tensor_scalar",
                             (cn[:, 4 * jj:4 * jj + 4], LT[:, 4 * jj:4 * jj + 4, h], Lref[:, jj, h:h + 1], -1.0,
                              ALU.subtract, ALU.mult), reads=[B_LT, B_Lref], writes=[Bcn])
                    rD, BrD = rdr_.get()
                    for bb in range(NB):
                        S.op("dve", "tensor_scalar",
                             (rD[:, bb * 128:(bb + 1) * 128], ident32, cn[:, bb:bb + 1], None, ALU.mult),
                             reads=[Bcn, B_c], writes=[BrD])
                seq = []
                for j in range(NQ):
                    nkb = (j + 1) * KB
                    for i in range(nkb):
                        seq.append((j, i, nkb))
                state = {}

                def issue_qk(idx):
                    j, i, nkb = seq[idx]
                    r = i - (nkb - KB)
                    c0 = max(r, 0) * 128
                    st, Bst = Sr.get()
                    if maps == 1:
                        S.op("pe", "matmul", (st[:, c0:512], kt[:, i * 128:(i + 1) * 128],
                                              qt[:, j * 512 + c0:(j + 1) * 512]),
                             dict(start=True, stop=False, skip_group_check=True), reads=[Bkt, Bqt], writes=[Bst])
                        S.op("pe", "matmul", (st[:, c0:512], ones_bf, rD[:, j * 512 + c0:(j + 1) * 512]),
                             dict(start=False, stop=(r < 0), skip_group_check=True), reads=[BrD, B_c], writes=[Bst])
                        if r >= 0:
                            S.op("pe", "matmul", (st[:, c0:c0 + 128], ident, trineg[:]),
                                 dict(start=False, stop=True, skip_group_check=True), reads=[B_c], writes=[Bst])
                    else:
                        for m in range(2):
                            S.op("pe", "matmul", (st[:, m * 256 + c0:(m + 1) * 256],
                                                  kt[:, i * 128:(i + 1) * 128],
                                                  qz[m][0][:, j * 256 + c0:(j + 1) * 256]),
                                 dict(start=True, stop=True, skip_group_check=True),
                                 reads=[Bkt, qz[m][1]], writes=[Bst])
                    state[idx] = (st, Bst, r, c0)

                LOOK = 2
                for idx in range(min(LOOK, len(seq))):
                    issue_qk(idx)
                cur = {}
                for idx in range(len(seq)):
                    j, i, nkb = seq[idx]
                    it_count += 1
                    if bg is not None and it_count % bg_every == 0:
                        bg_step()
                    if idx + LOOK < len(seq):
                        issue_qk(idx + LOOK)
                    st, Bst, r, c0 = state.pop(idx)
                    if i == 0:
                        ot, Bot = Or.get()
                        dn, Bdn = dr.get()
                        cur["o"] = (ot, Bot, dn, Bdn)
                        if l == 1:
                            def mk_bj(jj):
                                bj_, Bbj_ = bjr.get()
                                n_ = (jj + 1) * KB
                                S.op("dve", "tensor_scalar",
                                     (bj_[:, 0:n_], LT[:, 0:n_, h], Lref[:, jj, h:h + 1], None, ALU.subtract),
                                     reads=[B_LT, B_Lref], writes=[Bbj_])
                                return (bj_, Bbj_)
                            if j == 0:
                                cur["bj_next"] = mk_bj(0)
                            cur["bj"] = cur["bj_next"]
                            if j + 1 < NQ:
                                cur["bj_next"] = mk_bj(j + 1)
                    ot, Bot, dn, Bdn = cur["o"]
                    pt, Bpt = Pr.get()
                    if maps == 1:
                        bj, Bbj = cur["bj"]
                        S.op("act", "activation", (pt[:, c0:512], st[:, c0:512], AF.Exp),
                             dict(bias=bj[:, i:i + 1]), reads=[Bst, Bbj], writes=[Bpt])
                        S.op("pe", "matmul", (ot[:, c0:512], vt[:, i, :], pt[:, c0:512]),
                             dict(start=(i == 0), stop=(i == nkb - 1), skip_group_check=True),
                             reads=[Bvt, Bpt], writes=[Bot])
                        S.op("pe", "matmul", (dn[:, c0:512], ones_bf, pt[:, c0:512]),
                             dict(start=(i == 0), stop=(i == nkb - 1), skip_group_check=True),
                             reads=[Bpt, B_c], writes=[Bdn])
                    else:
                        st3 = st[:, :].rearrange("p (m q) -> p m q", m=2)
                        pt3 = pt[:, :].rearrange("p (m q) -> p m q", m=2)
                        S.op("act", "activation", (pt3[:, :, c0:256], st3[:, :, c0:256], AF.Exp),
                             reads=[Bst], writes=[Bpt])
                        if r >= 0:
                            for m in range(2):
                                a = m * 256 + c0
                                S.op("dve", "tensor_tensor", (pt[:, a:a + 128], pt[:, a:a + 128], tri_bf, ALU.mult),
                                     reads=[Bpt, B_c], writes=[Bpt])
                        for m in range(2):
                            a, b = m * 256 + c0, (m + 1) * 256
                            S.op("pe", "matmul", (ot[:, a:b], vt[:, i, :], pt[:, a:b]),
                                 dict(start=(i == 0 and m == 0), stop=(i == nkb - 1), skip_group_check=True),
                                 reads=[Bvt, Bpt], writes=[Bot])
                        if c0 == 0:
                            S.op("pe", "matmul", (dn[:, :], ones_bf, pt[:, :]),
                                 dict(start=(i == 0), stop=(i == nkb - 1), skip_group_check=True),
                                 reads=[Bpt, B_c], writes=[Bdn])
                        else:
                            for m in range(2):
                                a, b = m * 256 + c0, (m + 1) * 256
                                S.op("pe", "matmul", (dn[:, a:b], ones_bf, pt[:, a:b]),
                                     dict(start=False, stop=(i == nkb - 1), skip_group_check=True),
                                     reads=[Bpt, B_c], writes=[Bdn])
                    if i == nkb - 1:
                        rd, Brd = rdr.get()
                        S.op("dve", "reciprocal", (rd[:], dn[:]), reads=[Bdn], writes=[Brd])
                        if l == 1:
                            ost, Bos = osr.get()
                            S.op("dve", "tensor_tensor", (ost[:], ot[:], rd[:], ALU.mult),
                                 reads=[Bot, Brd], writes=[Bos])
                            S.dma("sp", aT_s[j][:, h, :], ost[:], reads=[Bos])
                        else:
                            o12, Bo12 = o12r.get()
                            S.op("dve", "tensor_tensor", (o12[:], ot[:], rd[:], ALU.mult),
                                 reads=[Bot, Brd], writes=[Bo12])
                            dd, Bdd = ddr.get()
                            S.op("dve", "scalar_tensor_tensor",
                                 (dd[:], o12[:, 256:512], prm[:, PNLAM:PNLAM + 1], o12[:, 0:256], ALU.mult, ALU.add),
                                 reads=[Bo12, B_prm], writes=[Bdd])
                            sq2, Bsq2 = sq2r.get()
                            S.op("act", "activation", (sq2[:], dd[:], AF.Square), reads=[Bdd], writes=[Bsq2])
                            ms, Bms = msr.get()
                            S.op("pe", "matmul", (ms[:, 0:256], ones_bf, sq2[:]), dict(start=True, stop=True),
                                 reads=[Bsq2, B_c], writes=[Bms])
                            rt2, Brt2 = rt2r.get()
                            S.op("act", "activation", (rt2[:], ms[:, 0:256], AF.Sqrt),
                                 dict(scale=1.0 / 128, bias=EPS), reads=[Bms], writes=[Brt2])
                            S.op("dve", "reciprocal", (rt2[:], rt2[:]), reads=[Brt2], writes=[Brt2])
                            ost, Bos = osr.get()
                            S.op("dve", "scalar_tensor_tensor",
                                 (ost[:, 0:256], dd[:], prm[:, PSUB:PSUB + 1], rt2[:], ALU.mult, ALU.mult),
                                 reads=[Bdd, Brt2, B_prm], writes=[Bos])
                            S.dma("sp", aT_s[j // 2][:, 8 + h, (j % 2) * 256:(j % 2 + 1) * 256],
                                  ost[:, 0:256], reads=[Bos])
            if bg is not None:
                while bg_step():
                    pass
        S.barrier()

    def phase_C(l, x_src, x_dst, w_out_t, Bout):
        with ExitStack() as es:
            p = f"C{l}"
            wr = Ring(es, nc, p + "w", [128, 16, 512], BF16, 3)
            uT = es.enter_context(nc.sbuf_tensor(p + "uT", [128, 64, 512], BF16))
            BuT = [Buf() for _ in range(64)]
            xt = es.enter_context(nc.sbuf_tensor(p + "xt", [128, 4, D], F32))
            Bx = [[Buf() for _ in range(4)] for _ in range(4)]
            xin = Ring(es, nc, p + "xin", [128, 4, 512], F32, 2)
            ah = es.enter_context(nc.sbuf_tensor(p + "ah", [128, 16, 512], BF16))
            ahB = [[Buf(), Buf()] for _ in range(4)]
            hbr = Ring(es, nc, p + "hb", [128, D], BF16, 1)
            junk = es.enter_context(nc.sbuf_tensor(p + "junk", [128, D], BF16))
            Bjunk = Buf()
            smr = Ring(es, nc, p + "sm", [128, 4], F32, 4)
            gB = es.enter_context(nc.sbuf_tensor(p + "g", [128, D], F32))
            BgB = Buf()
            S.dma("sp", gB[:], norm2_g[l:l + 1, :].partition_broadcast(128), writes=[BgB])
            rr = Ring(es, nc, p + "r", [128, 512], F32, 2)
            acc = Ring(es, nc, p + "acc", [128, 512], F32, 6, psum=True)
            tpr = Ring(es, nc, p + "tp", [128, 8, 128], BF16, 2, psum=True)
            all_ah = [b for tbb in ahB for b in tbb]
            wlist = []
            for ti in range(NT):
                wlist += [w_out_t[0, cb] for cb in range(4)]
                wlist += [wc_w1[l][0, cb] for cb in range(16)]
                wlist += [wc_w2[l][kg, cb] for cb in range(4) for kg in range(4)]
            wst = Stream(S, wr, wlist, 2)
            xist = Stream(S, xin, [x_src[ti * 512:(ti + 1) * 512, cb * 512:(cb + 1) * 512].rearrange(
                "(tb p) d -> p tb d", p=128) for ti in range(NT) for cb in range(4)], 1)
            S.dma("sp", ah[:], aT_s[0], writes=all_ah)
            for ti in range(NT):
                tok0 = ti * 512
                for cb in range(4):
                    wt, Bwt = wst.next()
                    xi, Bxi = xist.next()
                    for tb in range(4):
                        ps, Bps = acc.get()
                        for kc in range(16):
                            S.op("pe", "matmul", (ps[:], ah[:, kc, tb * 128:(tb + 1) * 128], wt[:, kc, :]),
                                 dict(start=(kc == 0), stop=(kc == 15)),
                                 reads=[Bwt, ahB[tb][kc // 8]], writes=[Bps])
                        S.op("dve", "tensor_tensor", (xt[:, tb, cb * 512:(cb + 1) * 512], ps[:], xi[:, tb, :], ALU.add),
                             reads=[Bps, Bxi], writes=[Bx[tb][cb]])
                for tb in range(4):
                    norm_block(p, xt[:, tb, :], Bx[tb] + [BgB], gB, hbr, junk, Bjunk, smr, tpr, ah, ahB, tb)
                for cb in range(16):
                    wt, Bwt = wst.next()
                    for oc in range(4):
                        ps, Bps = acc.get()
                        for kc in range(16):
                            S.op("pe", "matmul", (ps[:], wt[:, kc, oc * 128:(oc + 1) * 128], ah[:, kc, :]),
                                 dict(start=(kc == 0), stop=(kc == 15)),
                                 reads=[Bwt] + hT_reads(ahB, kc), writes=[Bps])
                        r, Br = rr.get()
                        S.op("act", "activation", (r[:], ps[:], AF.Relu), reads=[Bps], writes=[Br])
                        c = cb * 4 + oc
                        S.op("dve", "tensor_tensor", (uT[:, c, :], r[:], r[:], ALU.mult),
                             reads=[Br], writes=[BuT[c]])
                if ti + 1 < NT:
                    S.dma("sp", ah[:], aT_s[ti + 1], writes=all_ah)
                for cb in range(4):
                    pss = [acc.get() for _ in range(4)]
                    for kg in range(4):
                        wt, Bwt = wst.next()
                        for tb in range(4):
                            ps, Bps = pss[tb]
                            for kc in range(16):
                                c = kg * 16 + kc
                                S.op("pe", "matmul", (ps[:], uT[:, c, tb * 128:(tb + 1) * 128], wt[:, kc, :]),
                                     dict(start=(kg == 0 and kc == 0), stop=(kg == 3 and kc == 15)),
                                     reads=[Bwt, BuT[c]], writes=[Bps])
                    for tb in range(4):
                        ps, Bps = pss[tb]
                        sl = xt[:, tb, cb * 512:(cb + 1) * 512]
                        S.op("dve", "tensor_tensor", (sl, ps[:], sl, ALU.add),
                             reads=[Bps, Bx[tb][cb]], writes=[Bx[tb][cb]])
                    S.dma("sp", x_dst[tok0:tok0 + 512, cb * 512:(cb + 1) * 512].rearrange(
                        "(tb p) d -> p tb d", p=128), xt[:, :, cb * 512:(cb + 1) * 512],
                        reads=[Bx[tb][cb] for tb in range(4)], writes=[Bout])
        S.barrier()

    B_x1 = Buf("x1")
    B_out = Buf("out")
    if phases is None:
        phases = ["A0", "B0", "C0", "A1", "B1", "C1"]
    if "A0" in phases:
        phase_A(0, x_in)
    if "B0" in phases:
        phase_B(0)
    if "C0" in phases:
        phase_C(0, x_in, x1_s if "C1" in phases else out, wc_hout, B_x1 if "C1" in phases else B_out)
    if "A1" in phases:
        phase_A(1, x1_s)
    if "B1" in phases:
        phase_B(1)
    if "C1" in phases:
        phase_C(1, x1_s, out, wc_fout, B_out)
    S.finish([B_out])
    ges.close()
    return nc, S


def make_consts():
    c = np.zeros((128, 4, 128), np.float32)
    c[:, 0, :] = np.eye(128, dtype=np.float32)
    c[:, 1, :] = (np.arange(128)[None, :] >= np.arange(128)[:, None]).astype(np.float32)
    c[:, 2, :] = 1.0
    c[0:64, 3, 0:64] = 1.0
    c[64:128, 3, 64:128] = 1.0
    return c


PARAM_NAMES = ["norm1_g", "norm2_g", "hyb_w_in", "hyb_conv_w", "hyb_dq_g", "hyb_dk_g", "hyb_lq1", "hyb_lk1",
               "hyb_lq2", "hyb_lk2", "hyb_subln_g", "hyb_w_out", "fox_w_in", "fox_b_f", "fox_q_g", "fox_k_g",
               "fox_w_out", "mlp_w1", "mlp_w2"]


def kernel(**inputs):
    x = np.ascontiguousarray(inputs["x"], dtype=np.float32)
    B, T, _ = x.shape
    nc, _ = build(T)
    cst = make_consts()
    params = {k: np.ascontiguousarray(inputs[k], dtype=np.float32) for k in PARAM_NAMES}
    slots = [0, 1, None, None, 2, 3] if B == 4 else list(range(B))
    zeros = None
    in_maps = []
    for s in slots:
        if s is None:
            if zeros is None:
                zeros = {k: np.zeros_like(v) for k, v in params.items()}
                zeros["x"] = np.zeros_like(x[0])
                zeros["cst"] = np.zeros_like(cst)
            in_maps.append(zeros)
        else:
            m = dict(params)
            m["x"] = x[s]
            m["cst"] = cst
            in_maps.append(m)
    res = run_bass_kernel_spmd(nc, in_maps, core_ids=list(range(len(slots))))
    out = np.empty((B, T, D), np.float32)
    for c, s in enumerate(slots):
        if s is not None:
            out[s] = np.asarray(res.results[c]["out"])
    return out
```
